# Optimizing a Trainium2 kernel written in Bass

```python
import jax, jax.numpy as jnp
from jax import lax
import numpy as np

D_MODEL = 1024
BATCH = 8
SEQ = 4096
DEPTH = 4

GRID_W = 64
CTX_LEN = 256
N_MIXERS = 3
Q_BLOCK = 128
ROPE_THETA = 10000.0
NORM_EPS = 1e-6
DIFF_HEADS = 8
DIFF_HEAD_DIM = 64
GQA_HEADS = 8
GQA_KV_HEADS = 2
GQA_HEAD_DIM = 128
FNET_GROUPS = 4
FNET_GROUP_DIM = D_MODEL // FNET_GROUPS
D_FF = -(-8 * D_MODEL // (3 * 256)) * 256
N_A = len(range(0, DEPTH, N_MIXERS))
N_B = len(range(1, DEPTH, N_MIXERS))
N_C = len(range(2, DEPTH, N_MIXERS))

kernel_name = "hybrid_diffattn_gqa_fnet_prefix_dit"


def rms_norm(x, g):
    xf = x.astype(jnp.float32)
    y = xf * lax.rsqrt(jnp.mean(xf * xf, axis=-1, keepdims=True) + NORM_EPS)
    return (y * g.astype(jnp.float32)).astype(x.dtype)


def modulate(h, shift, scale):
    return h * (1 + scale) + shift


def lambda_init_fn(layer_idx):
    return 0.8 - 0.6 * float(np.exp(-0.3 * layer_idx))


def axial_rope_tables(rows, cols, head_dim):
    axis_dim = head_dim // 2
    inv_freq = ROPE_THETA ** (-jnp.arange(0, axis_dim, 2, dtype=jnp.float32) / axis_dim)
    ang = jnp.concatenate([rows[:, None].astype(jnp.float32) * inv_freq,
                           cols[:, None].astype(jnp.float32) * inv_freq], axis=-1)
    return jnp.cos(ang), jnp.sin(ang)


def apply_rope(x, cos, sin):
    shape = (cos.shape[0],) + (1,) * (x.ndim - 3) + (cos.shape[-1],)
    c_, s_ = cos.reshape(shape), sin.reshape(shape)
    xf = x.astype(jnp.float32)
    x1, x2 = xf[..., 0::2], xf[..., 1::2]
    out = jnp.stack([x1 * c_ - x2 * s_, x1 * s_ + x2 * c_], axis=-1).reshape(x.shape)
    return out.astype(x.dtype)


def sweep_blocks(fn, q):
    B, S = q.shape[:2]
    nb = S // Q_BLOCK
    qb = q.reshape((B, nb, Q_BLOCK) + q.shape[2:]).swapaxes(0, 1)
    out = lax.map(fn, qb)
    return out.swapaxes(0, 1).reshape((B, S) + out.shape[3:])


def swiglu(h, w_gu, w_down):
    g, u = jnp.split(h @ w_gu, 2, axis=-1)
    return (jax.nn.silu(g) * u) @ w_down


def diff_core(q, k, v, lam):
    s = jnp.einsum('bqhmd,bkhmd->bhmqk', q, k).astype(jnp.float32) * (DIFF_HEAD_DIM ** -0.5)
    p = jax.nn.softmax(s, axis=-1)
    a = p[:, :, 0] - lam * p[:, :, 1]
    return jnp.einsum('bhqk,bkhe->bqhe', a.astype(v.dtype), v)


def diff_attention(h, hc, w_qkv, lam_vecs, subln_g, w_o, lambda_init, rope, need_ctx):
    B, L, _ = h.shape
    C = hc.shape[1]
    dq = DIFF_HEADS * 2 * DIFF_HEAD_DIM
    cos, sin = rope

    def split_q(t, n):
        return t.reshape(B, n, DIFF_HEADS, 2, DIFF_HEAD_DIM)

    q, k, v = jnp.split(h @ w_qkv, 3, axis=-1)
    q = apply_rope(split_q(q, L), cos, sin)
    k = apply_rope(split_q(k, L), cos, sin)
    v = v.reshape(B, L, DIFF_HEADS, 2 * DIFF_HEAD_DIM)
    kc, vc = jnp.split(hc @ w_qkv[:, dq:], 2, axis=-1)
    kc = split_q(kc, C)
    vc = vc.reshape(B, C, DIFF_HEADS, 2 * DIFF_HEAD_DIM)

    lv = lam_vecs.astype(jnp.float32)
    lam = jnp.exp(jnp.sum(lv[0] * lv[1])) - jnp.exp(jnp.sum(lv[2] * lv[3])) + lambda_init

    def head_out(o, n):
        o = rms_norm(o, subln_g) * (1.0 - lambda_init)
        return o.reshape(B, n, DIFF_HEADS * 2 * DIFF_HEAD_DIM) @ w_o

    k_all = jnp.concatenate([kc, k], axis=1)
    v_all = jnp.concatenate([vc, v], axis=1)
    o = sweep_blocks(lambda qb: diff_core(qb, k_all, v_all, lam), q)
    y = head_out(o, L)
    yc = None
    if need_ctx:
        qc = split_q(hc @ w_qkv[:, :dq], C)
        yc = head_out(diff_core(qc, kc, vc, lam), C)
    return y, yc


def gqa_core(q, k, v):
    s = jnp.einsum('bqhgd,bkhd->bhgqk', q, k).astype(jnp.float32) * (GQA_HEAD_DIM ** -0.5)
    p = jax.nn.softmax(s, axis=-1)
    return jnp.einsum('bhgqk,bkhd->bqhgd', p.astype(v.dtype), v)


def gqa_attention(h, hc, w_qkv, q_norm, k_norm, w_o, rope, need_ctx):
    B, L, _ = h.shape
    C = hc.shape[1]
    G = GQA_HEADS // GQA_KV_HEADS
    dq = GQA_HEADS * GQA_HEAD_DIM
    dkv = GQA_KV_HEADS * GQA_HEAD_DIM
    cos, sin = rope

    def heads(t, n):
        q = rms_norm(t[..., :dq].reshape(B, n, GQA_KV_HEADS, G, GQA_HEAD_DIM), q_norm)
        k = rms_norm(t[..., dq:dq + dkv].reshape(B, n, GQA_KV_HEADS, GQA_HEAD_DIM), k_norm)
        v = t[..., dq + dkv:].reshape(B, n, GQA_KV_HEADS, GQA_HEAD_DIM)
        return q, k, v

    q, k, v = heads(h @ w_qkv, L)
    q = apply_rope(q, cos, sin)
    k = apply_rope(k, cos, sin)
    qc, kc, vc = heads(hc @ w_qkv, C)

    k_all = jnp.concatenate([kc, k], axis=1)
    v_all = jnp.concatenate([vc, v], axis=1)
    o = sweep_blocks(lambda qb: gqa_core(qb, k_all, v_all), q)
    y = o.reshape(B, L, dq) @ w_o
    yc = gqa_core(qc, kc, vc).reshape(B, C, dq) @ w_o if need_ctx else None
    return y, yc


def fourier_mix(h, w_o, b_o):
    B, L, D = h.shape
    hf = h.astype(jnp.float32).reshape(B, L, FNET_GROUPS, FNET_GROUP_DIM)
    f = jnp.fft.fft2(hf, axes=(1, 3), norm='ortho').real.astype(h.dtype).reshape(B, L, D)
    return f @ w_o + b_o


def setup_inputs(seed: int = 0) -> dict:
    key = jax.random.key(seed)
    ks = jax.random.split(key, 24)
    f32 = jnp.float32

    def nrm(k, shape, fan_in, s=1.0):
        return jax.random.normal(k, shape, f32) * (s * fan_in ** -0.5)

    def gain(k, shape):
        return 1.0 + 0.05 * jax.random.normal(k, shape, f32)

    D = D_MODEL
    return {
        "x": jax.random.normal(ks[0], (BATCH, SEQ, D), f32),
        "c": jax.random.normal(ks[1], (BATCH, D), f32),
        "ctx": jax.random.normal(ks[2], (BATCH, CTX_LEN, D), f32),
        "c_ctx": jax.random.normal(ks[3], (D,), f32),
        "mod_w": nrm(ks[4], (DEPTH, D, 6 * D), D, 0.5),
        "mod_b": 0.01 * jax.random.normal(ks[5], (DEPTH, 6 * D), f32),
        "ln_mix": gain(ks[6], (DEPTH, D)),
        "ln_ffn": gain(ks[7], (DEPTH, D)),
        "ffn_w_gu": nrm(ks[8], (DEPTH, D, 2 * D_FF), D),
        "ffn_w_down": nrm(ks[9], (DEPTH, D_FF, D), D_FF),
        "a_w_qkv": nrm(ks[10], (N_A, D, 3 * DIFF_HEADS * 2 * DIFF_HEAD_DIM), D),
        "a_lam": 0.1 * jax.random.normal(ks[11], (N_A, 4, DIFF_HEAD_DIM), f32),
        "a_subln": gain(ks[12], (N_A, 2 * DIFF_HEAD_DIM)),
        "a_w_o": nrm(ks[13], (N_A, DIFF_HEADS * 2 * DIFF_HEAD_DIM, D), D),
        "b_w_qkv": nrm(ks[14], (N_B, D, (GQA_HEADS + 2 * GQA_KV_HEADS) * GQA_HEAD_DIM), D),
        "b_q_norm": gain(ks[15], (N_B, GQA_HEAD_DIM)),
        "b_k_norm": gain(ks[16], (N_B, GQA_HEAD_DIM)),
        "b_w_o": nrm(ks[17], (N_B, GQA_HEADS * GQA_HEAD_DIM, D), GQA_HEADS * GQA_HEAD_DIM),
        "c_w_o": nrm(ks[18], (N_C, D, D), D),
        "c_b_o": 0.01 * jax.random.normal(ks[19], (N_C, D), f32),
        "final_norm": gain(ks[20], (D,)),
    }


def reference(x, c, ctx, c_ctx, mod_w, mod_b, ln_mix, ln_ffn, ffn_w_gu, ffn_w_down,
              a_w_qkv, a_lam, a_subln, a_w_o, b_w_qkv, b_q_norm, b_k_norm, b_w_o,
              c_w_o, c_b_o, final_norm):
    B, S, D = x.shape
    ROWS = S // GRID_W
    rows = jnp.repeat(jnp.arange(ROWS, dtype=jnp.int32), GRID_W)
    cols = jnp.tile(jnp.arange(GRID_W, dtype=jnp.int32), ROWS)
    rope_a = axial_rope_tables(rows, cols, DIFF_HEAD_DIM)
    rope_b = axial_rope_tables(rows, cols, GQA_HEAD_DIM)

    xc = ctx
    for i in range(DEPTH):
        last = i == DEPTH - 1
        kind, j = i % N_MIXERS, i // N_MIXERS
        sh1, sc1, g1, sh2, sc2, g2 = [m[:, None, :] for m in
                                      jnp.split(jax.nn.silu(c) @ mod_w[i] + mod_b[i], 6, axis=-1)]
        csh1, csc1, cg1, csh2, csc2, cg2 = jnp.split(jax.nn.silu(c_ctx) @ mod_w[i] + mod_b[i], 6, axis=-1)

        h = modulate(rms_norm(x, ln_mix[i]), sh1, sc1)
        if kind == 0:
            hc = modulate(rms_norm(xc, ln_mix[i]), csh1, csc1)
            y, yc = diff_attention(h, hc, a_w_qkv[j], a_lam[j], a_subln[j], a_w_o[j],
                                   lambda_init_fn(i), rope_a, not last)
        elif kind == 1:
            hc = modulate(rms_norm(xc, ln_mix[i]), csh1, csc1)
            y, yc = gqa_attention(h, hc, b_w_qkv[j], b_q_norm[j], b_k_norm[j], b_w_o[j],
                                  rope_b, not last)
        else:
            y = fourier_mix(h, c_w_o[j], c_b_o[j])
            yc = None
            if not last:
                hc = modulate(rms_norm(xc, ln_mix[i]), csh1, csc1)
                yc = fourier_mix(hc, c_w_o[j], c_b_o[j])

        x = x + g1 * y
        x = x + g2 * swiglu(modulate(rms_norm(x, ln_ffn[i]), sh2, sc2), ffn_w_gu[i], ffn_w_down[i])
        if not last:
            xc = xc + cg1 * yc
            xc = xc + cg2 * swiglu(modulate(rms_norm(xc, ln_ffn[i]), csh2, csc2),
                                   ffn_w_gu[i], ffn_w_down[i])

    return rms_norm(x, final_norm)
```

```python
import numpy as np
import ml_dtypes
from contextlib import ExitStack
import concourse.bass as bass
import concourse.mybir as mybir
from concourse.bass_utils import run_bass_kernel_spmd

F32 = mybir.dt.float32
BF16 = mybir.dt.bfloat16
AF = mybir.ActivationFunctionType
ALU = mybir.AluOpType
AX = mybir.AxisListType

D = 1024
L = 4096
C = 256
NTOK = L + C
NT = NTOK // 128
DFF = 2816
DEPTH = 4
EPS = 1e-6
GROUPS = [(g * 4, 4) for g in range(8)] + [(32, 2)]
NQKV_A = 3072
NQKV_B = 1536


def lambda_init_fn(i):
    return 0.8 - 0.6 * float(np.exp(-0.3 * i))


class T:
    __slots__ = ("name", "w", "re", "rd")

    def __init__(self, name=""):
        self.name = name
        self.w = None
        self.re = {}
        self.rd = []


class Sched:
    ENG = ("pe", "act", "dve", "pool", "sp")

    def __init__(self, nc, es, n_epochs=8, n_dma_sems=12):
        self.nc = nc
        self.eng = {"pe": nc.tensor, "act": nc.scalar, "dve": nc.vector,
                    "pool": nc.gpsimd, "sp": nc.sync}
        self.ops = []
        self.epoch = 0
        self.n_epochs = n_epochs
        self.csem = {}
        for e in ("pe", "act", "dve", "pool"):
            for ep in range(n_epochs):
                self.csem[(e, ep)] = es.enter_context(nc.semaphore(f"c_{e}_{ep}"))
        self.dsem = {}
        for q in ("sp", "pool"):
            self.dsem[q] = [es.enter_context(nc.semaphore(f"d_{q}_{i}")) for i in range(n_dma_sems)]
        self.last_on = {}
        self.dma_since = []
        self.pending_barrier = {}

    def new_epoch(self):
        if self.epoch + 1 < self.n_epochs:
            self.epoch += 1

    def op(self, engine, fn, reads=(), writes=(), dma=False):
        idx = len(self.ops)
        deps = set()
        for t in reads:
            if t.w is not None:
                deps.add(t.w)
        for t in writes:
            if t.w is not None:
                deps.add(t.w)
            deps.update(t.re.values())
            deps.update(t.rd)
        if engine in self.pending_barrier:
            deps.update(self.pending_barrier.pop(engine))
        for t in writes:
            t.w = idx
            t.re = {}
            t.rd = []
        for t in reads:
            if t.w == idx:
                continue
            if dma:
                t.rd.append(idx)
            else:
                t.re[engine] = idx
        deps.discard(idx)
        self.ops.append([engine, fn, deps, dma, False, self.epoch])
        if dma:
            self.dma_since.append(idx)
        else:
            self.last_on[engine] = idx
        return idx

    def barrier(self):
        deps = set(self.dma_since)
        deps.update(self.last_on.values())
        self.dma_since = []
        for e in self.ENG:
            s = self.pending_barrier.get(e, set())
            s.update(deps)
            self.pending_barrier[e] = s

    def emit(self, final_wait_engine="sp"):
        ops = self.ops
        self.barrier()
        fin = self.pending_barrier[final_wait_engine]
        ops.append([final_wait_engine, None, set(fin), False, False, self.epoch])
        for engine, fn, deps, dma, _, ep in ops:
            for d in deps:
                p = ops[d]
                if p[3]:
                    continue
                if p[0] == "pe" and engine == "pe" and not dma:
                    continue
                p[4] = True
        ccount = {k: 0 for k in self.csem}
        incval = {}
        dma_n = {"sp": 0, "pool": 0}
        dma_cnt = {}
        dma_info = {}
        waited = {e: {} for e in self.ENG}
        nwaits = 0
        for idx, (engine, fn, deps, dma, need_inc, ep) in enumerate(ops):
            E = self.eng[engine]
            w = waited[engine]
            need = {}
            for d in deps:
                p = ops[d]
                if p[3]:
                    sem, val = dma_info[d]
                else:
                    if p[0] == "pe" and engine == "pe" and not dma:
                        continue
                    sem, val = self.csem[(p[0], p[5])], incval[d]
                k = id(sem)
                if w.get(k, 0) < val and need.get(k, (None, 0))[1] < val:
                    need[k] = (sem, val)
            if dma:
                pool = self.dsem[engine]
                sidx = dma_n[engine] % len(pool)
                dma_n[engine] += 1
                sem = pool[sidx]
                k = id(sem)
                prev = dma_cnt.get(k, 0)
                if prev > 0 and w.get(k, 0) < prev and need.get(k, (None, 0))[1] < prev:
                    need[k] = (sem, prev)
                dma_cnt[k] = prev + 16
                dma_info[idx] = (sem, prev + 16)
            for k, (sem, val) in need.items():
                E.wait_ge(sem, val)
                w[k] = val
                nwaits += 1
            if fn is None:
                continue
            inst = fn()
            if dma:
                inst.then_inc(dma_info[idx][0], 16)
            elif need_inc:
                key = (engine, ep)
                ccount[key] += 1
                incval[idx] = ccount[key]
                inst.then_inc(self.csem[key], 1)
        self.stats = dict(n_ops=len(ops), n_waits=nwaits, counts={k: v for k, v in ccount.items() if v},
                          dma=dict(dma_n))
        return self.stats


class Arena:
    def __init__(self, ap_bf16, nbytes):
        self.ap = ap_bf16
        self.n = nbytes
        self.off = 0

    def alloc(self, shape, dt):
        esz = 4 if dt == F32 else 2
        free = int(np.prod(shape[1:]))
        nb = free * esz
        nb = (nb + 63) // 64 * 64
        assert self.off + nb <= self.n, (self.off, nb, self.n)
        v = self.ap[:, self.off // 2:(self.off + free * esz) // 2]
        if dt == F32:
            v = v.bitcast(F32)
        if len(shape) == 3:
            v = v.rearrange("p (a b) -> p a b", a=shape[1])
        elif len(shape) == 4:
            v = v.rearrange("p (a b c) -> p a b c", a=shape[1], b=shape[2])
        self.off += nb
        return v, T()

    def mark(self):
        return self.off

    def release(self, m):
        self.off = m


def build_nc(layers=(0, 1, 2, 3), dbg=False):
    nc = bass.Bass("TRN2", target_bir_lowering=False)

    def din(name, shape, dt=F32):
        return nc.dram_tensor(name, list(shape), dt, kind="ExternalInput").ap()

    def dscr(name, shape, dt):
        return nc.dram_tensor(name, list(shape), dt, kind="Internal").ap()

    x_in = din("x", [L, D])
    ctx_in = din("ctx", [C, D])
    ccol_in = din("ccol", [128, 8, 2])
    mod_w = din("mod_w", [DEPTH, D, 6 * D])
    mod_b = din("mod_b", [DEPTH, 6 * D])
    ln_mix = din("ln_mix", [DEPTH, D])
    ln_ffn = din("ln_ffn", [DEPTH, D])
    ffn_w_gu = din("ffn_w_gu", [DEPTH, D, 2 * DFF])
    ffn_w_down = din("ffn_w_down", [DEPTH, DFF, D])
    a_w_qkv = din("a_w_qkv", [2, D, NQKV_A])
    a_lam = din("a_lam", [2, 256])
    a_subln = din("a_subln", [2, 128])
    a_w_o = din("a_w_o", [2, D, D])
    b_w_qkv = din("b_w_qkv", [1, D, NQKV_B])
    b_q_norm = din("b_q_norm", [1, 128])
    b_k_norm = din("b_k_norm", [1, 128])
    b_w_o = din("b_w_o", [1, D, D])
    c_w_o = din("c_w_o", [1, D, D])
    c_b_o = din("c_b_o", [1, D])
    final_norm = din("final_norm", [D])
    ident_in = din("ident", [128, 128], BF16)
    ropeA_in = din("ropeA", [2, 128, 32, 32])
    ropeB_in = din("ropeB", [2, 128, 32, 64])
    dftL_in = din("dftL", [2, L, L], BF16)
    dft256_in = din("dft256", [3, 128, 2, 256], BF16)
    out = nc.dram_tensor("out", [L, D], F32, kind="ExternalOutput").ap()

    xs = dscr("xs", [NTOK, D], F32)
    modsd = dscr("modsd", [2, 6 * D], F32)
    qT_d = dscr("qT_d", [8, 128, NTOK], BF16)
    kT_d = dscr("kT_d", [8, 128, NTOK], BF16)
    v_d = dscr("v_d", [128, NT, 8, 130], BF16)
    oT_d = dscr("oT_d", [8, 128, NTOK], BF16)
    g_d = dscr("g_d", [NTOK, 2048], BF16)

    es = ExitStack()
    with es:
        S = Sched(nc, es, n_epochs=9, n_dma_sems=12)
        ARENA_BYTES = 212000
        arena_t = es.enter_context(nc.sbuf_tensor("arena", [128, ARENA_BYTES // 2], BF16))
        AR = Arena(arena_t, ARENA_BYTES)
        psum = es.enter_context(nc.psum_tensor("psum", [128, 8, 512], F32))
        PB = [T(f"bank{i}") for i in range(8)]

        def bank_bf(i):
            return psum[:, i, :].bitcast(BF16)

        def mm(out_ap, lhsT, rhs, start, stop, R, W, skip=False):
            if skip:
                S.op("pe", lambda: nc.tensor.matmul(out_ap, lhsT=lhsT, rhs=rhs, start=start, stop=stop,
                                                    skip_group_check=True), reads=R, writes=W)
            else:
                S.op("pe", lambda: nc.tensor.matmul(out_ap, lhsT=lhsT, rhs=rhs, start=start, stop=stop),
                     reads=R, writes=W)

        def tr(out_ap, in_ap, R, W):
            S.op("pe", lambda: nc.tensor.transpose(out_ap, in_ap, ident[:]), reads=list(R) + [t_ident], writes=W)

        def dma(q, out_ap, in_ap, R, W):
            e = nc.sync if q == "sp" else nc.gpsimd
            S.op(q, lambda: e.dma_start(out=out_ap, in_=in_ap), reads=R, writes=W, dma=True)

        def act(out_ap, in_ap, func, R, W, scale=1.0, bias=0.0, accum=None):
            if accum is None:
                S.op("act", lambda: nc.scalar.activation(out=out_ap, in_=in_ap, func=func, scale=scale, bias=bias),
                     reads=R, writes=W)
            else:
                S.op("act", lambda: nc.scalar.activation(out=out_ap, in_=in_ap, func=func, scale=scale, bias=bias,
                                                         accum_out=accum), reads=R, writes=W)

        def tt(eng, out_ap, a, b, op, R, W):
            e = nc.vector if eng == "dve" else nc.gpsimd
            S.op(eng, lambda: e.tensor_tensor(out=out_ap, in0=a, in1=b, op=op), reads=R, writes=W)

        def ts(eng, out_ap, a, s1, op0, R, W, s2=None, op1=None):
            e = nc.vector if eng == "dve" else nc.gpsimd
            if op1 is None:
                S.op(eng, lambda: e.tensor_scalar(out=out_ap, in0=a, scalar1=s1, scalar2=None, op0=op0), reads=R, writes=W)
            else:
                S.op(eng, lambda: e.tensor_scalar(out=out_ap, in0=a, scalar1=s1, scalar2=s2, op0=op0, op1=op1),
                     reads=R, writes=W)

        def stt(eng, out_ap, a, sc, b, op0, op1, R, W):
            e = nc.vector if eng == "dve" else nc.gpsimd
            S.op(eng, lambda: e.scalar_tensor_tensor(out=out_ap, in0=a, scalar=sc, in1=b, op0=op0, op1=op1),
                 reads=R, writes=W)

        def recip(out_ap, in_ap, R, W):
            S.op("dve", lambda: nc.vector.reciprocal(out=out_ap, in_=in_ap), reads=R, writes=W)

        def copy(eng, out_ap, in_ap, R, W):
            if eng == "act":
                S.op("act", lambda: nc.scalar.copy(out=out_ap, in_=in_ap), reads=R, writes=W)
            else:
                e = nc.vector if eng == "dve" else nc.gpsimd
                S.op(eng, lambda: e.tensor_copy(out=out_ap, in_=in_ap), reads=R, writes=W)

        def memset(eng, ap, val, W):
            e = nc.vector if eng == "dve" else nc.gpsimd
            S.op(eng, lambda: e.memset(ap, val), writes=W)

        ident, t_ident = AR.alloc([128, 128], BF16)
        scol, t_scol = AR.alloc([128, 8, 2], F32)
        stat, t_stat = AR.alloc([128, 64], F32)
        Ab, t_Ab = AR.alloc([128, D], F32)
        shb, t_shb = AR.alloc([128, D], F32)
        gb, t_gb = AR.alloc([128, D], F32)
        junk, t_junk = AR.alloc([128, D], BF16)
        BASE = AR.mark()

        dma("sp", ident, ident_in, [], [t_ident])
        dma("sp", scol, ccol_in, [], [t_scol])
        act(scol, scol, AF.Silu, [t_scol], [t_scol])

        stat_slots = [(stat[:, 4 * i:4 * i + 4], T()) for i in range(16)]
        stat_i = [0]

        def next_stat():
            s = stat_slots[stat_i[0] % 16]
            stat_i[0] += 1
            return s

        def xsrc(layer, t0, gs):
            if layer == layers[0] and layer == 0:
                if t0 < 32:
                    a = x_in[t0 * 128:(t0 + gs) * 128, :]
                else:
                    a = ctx_in[(t0 - 32) * 128:(t0 - 32 + gs) * 128, :]
            else:
                a = xs[t0 * 128:(t0 + gs) * 128, :]
            return a.rearrange("(j p) d -> p j d", p=128)

        def prep_mod(i, kind, row, scr, need_g=True):
            ln = ln_mix if kind == 1 else ln_ffn
            o = 0 if kind == 1 else 3
            lnb, t_lnb = scr
            dma("sp", lnb, ln[i, :].partition_broadcast(128), [], [t_lnb])
            dma("sp", Ab, modsd[row, (o + 1) * D:(o + 2) * D].partition_broadcast(128), [t_modsd], [t_Ab])
            dma("sp", shb, modsd[row, o * D:(o + 1) * D].partition_broadcast(128), [t_modsd], [t_shb])
            if need_g:
                dma("sp", gb, modsd[row, (o + 2) * D:(o + 3) * D].partition_broadcast(128), [t_modsd], [t_gb])
            stt("dve", Ab, Ab, 1.0, lnb, ALU.add, ALU.mult, [t_Ab, t_lnb], [t_Ab])

        t_modsd = T("modsd")
        t_xs = [T() for _ in range(NT)]

        def rms_rstd(x_ap, t_x, n):
            st, t_st = next_stat()
            act(junk[:, 0:n], x_ap, AF.Square, [t_x], [t_junk, t_st], accum=st[:, 0:1])
            act(st[:, 1:2], st[:, 0:1], AF.Ln, [t_st], [t_st], scale=1.0 / n, bias=EPS)
            act(st[:, 1:2], st[:, 1:2], AF.Exp, [t_st], [t_st], scale=-0.5)
            return st[:, 1:2], t_st

        def norm_mod_A(x_ap, t_x, t32, t_t32, hbf, t_hbf):
            r, t_r = rms_rstd(x_ap, t_x, D)
            stt("dve", t32, x_ap, r, Ab, ALU.mult, ALU.mult, [t_x, t_r, t_Ab], [t_t32])
            tt("pool", hbf, t32, shb, ALU.add, [t_t32, t_shb], [t_hbf])

        def norm_mod_B(hbf, t_hbf, bank, hT_dst, t_hT):
            pb = bank_bf(bank)
            for k in range(8):
                tr(pb[:, k * 128:(k + 1) * 128], hbf[:, k * 128:(k + 1) * 128], [t_hbf], [PB[bank]])
            copy("act", hT_dst, pb.rearrange("p (k t) -> p k t", k=8), [PB[bank]], [t_hT])

        def norm_mod_T(x_ap, t_x, t32, t_t32, hbf, t_hbf, bank, hT_dst, t_hT):
            r, t_r = rms_rstd(x_ap, t_x, D)
            stt("dve", t32, x_ap, r, Ab, ALU.mult, ALU.mult, [t_x, t_r, t_Ab], [t_t32])
            tt("pool", hbf, t32, shb, ALU.add, [t_t32, t_shb], [t_hbf])
            pb = bank_bf(bank)
            for k in range(8):
                tr(pb[:, k * 128:(k + 1) * 128], hbf[:, k * 128:(k + 1) * 128], [t_hbf], [PB[bank]])
            copy("act", hT_dst, pb.rearrange("p (k t) -> p k t", k=8), [PB[bank]], [t_hT])

        def phase_mod(i):
            m0 = AR.mark()
            wm = [AR.alloc([128, 8, 512], F32) for _ in range(2)]
            mods, t_mods = AR.alloc([128, 6 * D], F32)
            modb, t_modb = AR.alloc([128, 6 * D], F32)
            dma("sp", modb[0:2, :], mod_b[i, :].partition_broadcast(2), [], [t_modb])
            wv = mod_w[i].rearrange("(k p) n -> p k n", p=128)
            for n in range(12):
                buf, t_buf = wm[n % 2]
                dma("sp", buf, wv[:, :, n * 512:(n + 1) * 512], [], [t_buf])
                b = n % 2
                for k in range(8):
                    mm(psum[0:2, b, :], scol[:, k, :], buf[:, k, :], k == 0, k == 7, [t_scol, t_buf], [PB[b]])
                tt("dve", mods[0:2, n * 512:(n + 1) * 512], psum[0:2, b, :], modb[0:2, n * 512:(n + 1) * 512],
                   ALU.add, [PB[b], t_modb], [t_mods])
            dma("sp", modsd, mods[0:2, :], [t_mods], [t_modsd])
            S.barrier()
            AR.release(m0)

        def phase_qkv(i, kind, j, last):
            m0 = AR.mark()
            diff = kind == 0
            nqkv = NQKV_A if diff else NQKV_B
            wsrc = (a_w_qkv if diff else b_w_qkv)[j].rearrange("(k p) n -> p k n", p=128)
            W, t_W = AR.alloc([128, 8, nqkv], BF16)
            for k in range(8):
                dma("pool", W[:, k, :], wsrc[:, k, :], [], [t_W])
            npair = 32 if diff else 64
            rope_src = ropeA_in if diff else ropeB_in
            cosT, t_cos = AR.alloc([128, 32, npair], F32)
            sinT, t_sin = AR.alloc([128, 32, npair], F32)
            dma("sp", cosT, rope_src[0], [], [t_cos])
            dma("sp", sinT, rope_src[1], [], [t_sin])
            nkh = 8 if diff else 2
            xg = [AR.alloc([128, 4, D], F32) for _ in range(2)]
            t32, t_t32 = AR.alloc([128, D], F32)
            hbfs = [AR.alloc([128, D], BF16) for _ in range(2)]
            hTs = [AR.alloc([128, 8, 512], BF16) for _ in range(2)]
            rtmp = [AR.alloc([128, 256], F32) for _ in range(4)]
            qtm = [AR.alloc([128, 512], BF16) for _ in range(2)]
            qst = [AR.alloc([128, 8, 512], BF16) for _ in range(2)]
            kst = [AR.alloc([128, nkh, 512], BF16) for _ in range(2)]
            vst = [AR.alloc([128, 4, nkh, 130], BF16) for _ in range(2)]
            for v_, t_v in vst:
                memset("pool", v_, 1.0, [t_v])
            if not diff:
                sq, t_sq = AR.alloc([128, 512], F32)
                nrm, t_nrm = AR.alloc([128, 512], F32)
                gq, t_gq = AR.alloc([128, 128], F32)
                gk, t_gk = AR.alloc([128, 128], F32)
                dma("sp", gq, b_q_norm[0, :].partition_broadcast(128), [], [t_gq])
                dma("sp", gk, b_k_norm[0, :].partition_broadcast(128), [], [t_gk])
            if diff:
                blocks = [[("q", 0, 512, 0)], [("q", 0, 512, 4)], [("k", 0, 512, 0)], [("k", 0, 512, 4)],
                          [("v", 0, 512, 0)], [("v", 0, 512, 4)]]
            else:
                blocks = [[("q", 0, 512, 0)], [("q", 0, 512, 4)], [("k", 0, 256, 0), ("v", 256, 256, 0)]]
            obank = [2, 3, 4, 5]
            cnt = dict(oi=0, ti=0, rope=0)
            hbf4 = hbfs + [AR.alloc([128, D], BF16) for _ in range(2)]
            live = [g_ for g_ in GROUPS]

            def loadA(gi):
                t0, gs = live[gi]
                xgb, t_xg = xg[gi % 2]
                dma("sp", xgb[:, 0:gs, :], xsrc(i, t0, gs), t_xs[t0:t0 + gs], [t_xg])

            def partA(gi):
                t0, gs = live[gi]
                isctx = t0 >= 32
                if gi == 0 or isctx:
                    prep_mod(i, 1, 1 if isctx else 0, (t32, t_t32), need_g=False)
                xgb, t_xg = xg[gi % 2]
                for jj in range(gs):
                    hbf, t_hbf = hbf4[jj]
                    norm_mod_A(xgb[:, jj, :], t_xg, t32, t_t32, hbf, t_hbf)

            def partB(gi):
                t0, gs = live[gi]
                hT, t_hT = hTs[gi % 2]
                for jj in range(gs):
                    hbf, t_hbf = hbf4[jj]
                    norm_mod_B(hbf, t_hbf, cnt["ti"] % 2, hT[:, :, jj * 128:(jj + 1) * 128], t_hT)
                    cnt["ti"] += 1

            nrm3 = None
            if not diff:
                sq2 = [(sq, t_sq), AR.alloc([128, 512], F32)]
                nrm3 = [(nrm, t_nrm), AR.alloc([128, 512], F32)]
            rsets = [rtmp, [AR.alloc([128, 256], F32) for _ in range(4)]]
            qtm3 = qtm + [AR.alloc([128, 512], BF16)]
            rot = dict(sq=0, nrm=0, rs=0, qtm=0, tb=0)

            def make_entries(gi, jj, n, segs, ob):
                t0, gs = live[gi]
                isctx = t0 >= 32
                tok_tile = t0 + jj
                q_st, t_qst = qst[gi % 2]
                k_st, t_kst = kst[gi % 2]
                v_st, t_vst = vst[gi % 2]
                entries = []
                for (sk, c0, ncol, h0) in segs:
                    psrc = psum[:, ob, c0:c0 + ncol]
                    t_psrc = PB[ob]
                    if sk == "v":
                        def s_v(psrc=psrc, t_psrc=t_psrc, ncol=ncol, h0=h0):
                            nh = ncol // 128
                            copy("act", v_st[:, jj, h0:h0 + nh, 0:128],
                                 psrc.rearrange("p (h d) -> p h d", h=nh), [t_psrc], [t_vst])
                        entries.append([s_v])
                        continue
                    stages = []
                    cur = dict(src=psrc, t=t_psrc)
                    nh = ncol // 128
                    if not diff:
                        g_ap, t_g = (gq, t_gq) if sk == "q" else (gk, t_gk)
                        st, t_st = next_stat()
                        sq_, t_sq_ = sq2[rot["sq"] % 2]
                        rot["sq"] += 1
                        nr_, t_nr_ = nrm3[rot["nrm"] % 2]
                        rot["nrm"] += 1

                        def s_sq(psrc=psrc, t_psrc=t_psrc, ncol=ncol, nh=nh, st=st, t_st=t_st, sq_=sq_, t_sq_=t_sq_):
                            act(sq_[:, 0:ncol], psrc, AF.Square, [t_psrc], [t_sq_])
                            S.op("dve", lambda o=st[:, 0:nh], a_=sq_[:, 0:ncol].rearrange("p (h d) -> p h d", h=nh):
                                 nc.vector.tensor_reduce(out=o, in_=a_, axis=AX.X, op=ALU.add),
                                 reads=[t_sq_], writes=[t_st])
                            act(st[:, 0:nh], st[:, 0:nh], AF.Ln, [t_st], [t_st], scale=1.0 / 128, bias=EPS)
                            act(st[:, 0:nh], st[:, 0:nh], AF.Exp, [t_st], [t_st], scale=-0.5)

                        def s_nrm(psrc=psrc, t_psrc=t_psrc, ncol=ncol, nh=nh, st=st, t_st=t_st, nr_=nr_, t_nr_=t_nr_,
                                  g_ap=g_ap, t_g=t_g):
                            tt("dve", nr_[:, 0:ncol].rearrange("p (h d) -> p h d", h=nh),
                               psrc.rearrange("p (h d) -> p h d", h=nh),
                               st[:, 0:nh].unsqueeze(2).to_broadcast([128, nh, 128]), ALU.mult,
                               [t_psrc, t_st], [t_nr_])
                            tt("pool", nr_[:, 0:ncol].rearrange("p (h d) -> p h d", h=nh),
                               nr_[:, 0:ncol].rearrange("p (h d) -> p h d", h=nh),
                               g_ap[:, :].unsqueeze(1).to_broadcast([128, nh, 128]), ALU.mult,
                               [t_nr_, t_g], [t_nr_])
                        stages += [s_sq, s_nrm]
                        cur = dict(src=nr_[:, 0:ncol], t=t_nr_)
                    q_tm, t_qtm = qtm3[rot["qtm"] % 3]
                    rot["qtm"] += 1
                    src, t_src = cur["src"], cur["t"]
                    if isctx:
                        def s_cp(src=src, t_src=t_src, q_tm=q_tm, t_qtm=t_qtm, ncol=ncol):
                            copy("dve", q_tm[:, 0:ncol], src, [t_src], [t_qtm])
                        stages.append(s_cp)
                    else:
                        rset = rsets[rot["rs"] % 2]
                        rot["rs"] += 1
                        nhr = ncol // (2 * npair)
                        sv = src.rearrange("p (h i two) -> p h i two", h=nhr, two=2)
                        dv = q_tm[:, 0:ncol].rearrange("p (h i two) -> p h i two", h=nhr, two=2)
                        cb = cosT[:, tok_tile, :].unsqueeze(1).to_broadcast([128, nhr, npair])
                        sb_ = sinT[:, tok_tile, :].unsqueeze(1).to_broadcast([128, nhr, npair])
                        rv = [(r_[:, 0:ncol // 2].rearrange("p (h i) -> p h i", h=nhr), t_r) for r_, t_r in rset]

                        def s_rm(sv=sv, cb=cb, sb_=sb_, rv=rv, t_src=t_src):
                            tt("dve", rv[0][0], sv[:, :, :, 0], cb, ALU.mult, [t_src, t_cos], [rv[0][1]])
                            tt("dve", rv[1][0], sv[:, :, :, 1], sb_, ALU.mult, [t_src, t_sin], [rv[1][1]])
                            tt("dve", rv[2][0], sv[:, :, :, 0], sb_, ALU.mult, [t_src, t_sin], [rv[2][1]])
                            tt("dve", rv[3][0], sv[:, :, :, 1], cb, ALU.mult, [t_src, t_cos], [rv[3][1]])

                        def s_ra(dv=dv, rv=rv, t_qtm=t_qtm):
                            tt("pool", dv[:, :, :, 0], rv[0][0], rv[1][0], ALU.subtract, [rv[0][1], rv[1][1]], [t_qtm])
                            tt("pool", dv[:, :, :, 1], rv[2][0], rv[3][0], ALU.add, [rv[2][1], rv[3][1]], [t_qtm])
                        stages += [s_rm, s_ra]
                    tb = 6 + (rot["tb"] % 2)
                    rot["tb"] += 1

                    def s_tr(q_tm=q_tm, t_qtm=t_qtm, nh=nh, tb=tb, sk=sk, h0=h0):
                        pb = bank_bf(tb)
                        for hh in range(nh):
                            tr(pb[:, hh * 128:(hh + 1) * 128], q_tm[:, hh * 128:(hh + 1) * 128], [t_qtm], [PB[tb]])
                        dst, t_dst = (q_st, t_qst) if sk == "q" else (k_st, t_kst)
                        copy("act", dst[:, h0:h0 + nh, jj * 128:(jj + 1) * 128],
                             pb[:, 0:nh * 128].rearrange("p (h t) -> p h t", h=nh), [PB[tb]], [t_dst])
                    stages.append(s_tr)
                    entries.append(stages)
                return entries

            inflight = []

            def pump(drain=False):
                while True:
                    for e_ in list(inflight):
                        e_.pop(0)()
                        if not e_:
                            inflight.remove(e_)
                    if not drain or not inflight:
                        break

            fins = []
            ngr = len(live)
            loadA(0)
            loadA(1)
            partA(0)
            partB(0)
            for gi, (t0, gs) in enumerate(live):
                isctx = t0 >= 32
                Wd = gs * 128
                hT, t_hT = hTs[gi % 2]
                q_st, t_qst = qst[gi % 2]
                k_st, t_kst = kst[gi % 2]
                v_st, t_vst = vst[gi % 2]
                if gi + 1 < ngr:
                    partA(gi + 1)
                for jj in range(gs):
                    for n, segs in enumerate(blocks):
                        if last and isctx and segs[0][0] == "q":
                            continue
                        ob = obank[cnt["oi"] % 4]
                        cnt["oi"] += 1
                        for k in range(8):
                            mm(psum[:, ob, :], hT[:, k, jj * 128:(jj + 1) * 128], W[:, k, n * 512:(n + 1) * 512],
                               k == 0, k == 7, [t_hT, t_W], [PB[ob]])
                        inflight.extend(make_entries(gi, jj, n, segs, ob))
                        pump()
                pump(drain=True)
                if gi + 1 < ngr:
                    partB(gi + 1)
                if gi + 2 < ngr:
                    loadA(gi + 2)
                c0 = t0 * 128
                if not (last and isctx):
                    dma("sp", qT_d[:, :, c0:c0 + Wd].rearrange("h p t -> p h t"), q_st[:, :, 0:Wd], [t_qst], [t_qTd])
                dma("sp", kT_d[0:nkh, :, c0:c0 + Wd].rearrange("h p t -> p h t"), k_st[:, :, 0:Wd], [t_kst], [t_kTd])
                dma("sp", v_d[:, t0:t0 + gs, 0:nkh, :], v_st[:, 0:gs, :, :], [t_vst], [t_vd])
            S.barrier()
            AR.release(m0)

        t_qTd, t_kTd, t_vd, t_oTd, t_gd = T(), T(), T(), T(), T()

        ffn_w = {}

        def load_ffn_weights(i):
            wgu, t_wgu = AR.alloc([128, 8, 2 * DFF], BF16)
            wdn, t_wdn = AR.alloc([128, 22, D], BF16)
            src = ffn_w_gu[i].rearrange("(k p) n -> p k n", p=128)
            for k in range(8):
                dma("pool", wgu[:, k, :], src[:, k, :], [], [t_wgu])
            srcd = ffn_w_down[i].rearrange("(m p) n -> p m n", p=128)
            for m in range(0, 22, 6):
                me = min(22, m + 6)
                dma("pool", wdn[:, m:me, :], srcd[:, m:me, :], [], [t_wdn])
            ffn_w["gu"] = (wgu, t_wgu)
            ffn_w["dn"] = (wdn, t_wdn)

        def phase_att(i, kind, j, last):
            diff = kind == 0
            m0 = AR.mark()
            lam_init = lambda_init_fn(i)
            KT = [AR.alloc([128, NTOK], BF16) for _ in range(2)]
            VA = [AR.alloc([128, NT, 130], BF16) for _ in range(2)]
            QT = [AR.alloc([128, NTOK], BF16)]
            Pb = [AR.alloc([128, 2, 512], BF16) for _ in range(2)]
            o32 = [AR.alloc([128, 128], F32) for _ in range(4)]
            obf = [AR.alloc([128, 128], BF16) for _ in range(4)]
            OTst = [AR.alloc([128, 512], BF16) for _ in range(2)]
            if diff:
                lamt, t_lam = AR.alloc([128, 256], F32)
                gsub, t_gsub = AR.alloc([128, 128], F32)
                lst, t_lst = AR.alloc([128, 8], F32)
                dma("sp", lamt, a_lam[j, :].partition_broadcast(128), [], [t_lam])
                dma("sp", gsub, a_subln[j, :].partition_broadcast(128), [], [t_gsub])
                tt("dve", lamt[:, 0:64], lamt[:, 0:64], lamt[:, 64:128], ALU.mult, [t_lam], [t_lam])
                tt("dve", lamt[:, 128:192], lamt[:, 128:192], lamt[:, 192:256], ALU.mult, [t_lam], [t_lam])
                S.op("dve", lambda: nc.vector.tensor_reduce(out=lst[:, 0:1], in_=lamt[:, 0:64], axis=AX.X, op=ALU.add),
                     reads=[t_lam], writes=[t_lst])
                S.op("dve", lambda: nc.vector.tensor_reduce(out=lst[:, 1:2], in_=lamt[:, 128:192], axis=AX.X, op=ALU.add),
                     reads=[t_lam], writes=[t_lst])
                act(lst[:, 2:4], lst[:, 0:2], AF.Exp, [t_lst], [t_lst])
                tt("dve", lst[:, 4:5], lst[:, 3:4], lst[:, 2:3], ALU.subtract, [t_lst], [t_lst])
                ts("dve", lst[:, 5:6], lst[:, 4:5], -lam_init, ALU.add, [t_lst], [t_lst])
                nlam = lst[:, 5:6]
                ts("dve", gsub, gsub, 1.0 - lam_init, ALU.mult, [t_gsub], [t_gsub])
            scale = (64 ** -0.5) if diff else (128 ** -0.5)
            nacc = 8 if diff else 4

            def acc_ap(idx):
                return psum[:, 4 + idx // 3, (idx % 3) * 130:(idx % 3) * 130 + 130], PB[4 + idx // 3]

            stage, t_stage = AR.alloc([128, 3, 390], F32)

            def st_ap(idx):
                return stage[:, idx // 3, (idx % 3) * 130:(idx % 3) * 130 + 130]

            cnt = dict(sg=0, pb=0, st=0)
            for h in range(8):
                s = h % 2
                hk = h if diff else h // 4
                (KTs, t_KT), (VAs, t_VA), (QTs, t_QT) = KT[s], VA[s], QT[0]
                if h == 0:
                    dma("sp", KTs, kT_d[hk], [t_kTd], [t_KT])
                    dma("sp", VAs, v_d[:, :, hk, :], [t_vd], [t_VA])
                dma("sp", QTs, qT_d[h], [t_qTd], [t_QT])
                if h + 1 < 8:
                    hk1 = (h + 1) if diff else (h + 1) // 4
                    dma("sp", KT[1 - s][0], kT_d[hk1], [t_kTd], [KT[1 - s][1]])
                    dma("sp", VA[1 - s][0], v_d[:, :, hk1, :], [t_vd], [VA[1 - s][1]])
                chunks = [(qc * 512, 4, False) for qc in range(8)]
                if not last:
                    chunks.append((L, 2, True))
                allsteps = []
                for (q0, nq, isctx) in chunks:
                    kts = [32, 33] if isctx else list(range(NT))
                    if diff:
                        steps = [((kt, 0, 64), (kt, 64, 128)) for kt in kts]
                    else:
                        steps = [((kts[a_], 0, 128), (kts[a_ + 1], 0, 128)) for a_ in range(0, len(kts), 2)]
                    for si, (sa, sb2) in enumerate(steps):
                        allsteps.append(dict(q0=q0, nq=nq, sa=sa, sb=sb2, first=si == 0, last=si == len(steps) - 1))

                def emit_qk_exp(stp):
                    g = cnt["sg"] % 2
                    cnt["sg"] += 1
                    b0 = 2 * g
                    Wq = stp["nq"] * 128
                    q0 = stp["q0"]
                    for half, (kt, r0, r1) in enumerate((stp["sa"], stp["sb"])):
                        mm(psum[:, b0 + half, 0:Wq], KTs[r0:r1, kt * 128:(kt + 1) * 128], QTs[r0:r1, q0:q0 + Wq],
                           True, True, [t_KT, t_QT], [PB[b0 + half]])
                    P, t_P = Pb[cnt["pb"] % 2]
                    cnt["pb"] += 1
                    act(P[:, :, 0:Wq], psum[:, b0:b0 + 2, 0:Wq], AF.Exp, [PB[b0], PB[b0 + 1]], [t_P], scale=scale)
                    stp["P"] = (P, t_P)

                def emit_pv(stp):
                    P, t_P = stp["P"]
                    nq = stp["nq"]
                    for half, (kt, r0, r1) in enumerate((stp["sa"], stp["sb"])):
                        for jq in range(nq):
                            idx = (half * nq + jq) if diff else jq
                            a_ap, t_a = acc_ap(idx)
                            mm(a_ap, P[:, half, jq * 128:(jq + 1) * 128], VAs[:, kt, :], False, False,
                               [t_P, t_VA], [t_a], skip=True)

                def emit_evac(stp):
                    nq = stp["nq"]
                    q0 = stp["q0"]
                    Wq = nq * 128
                    nb = 3 if (diff and nq == 4) else 2
                    copy("dve", stage[:, 0:nb, :], psum[:, 4:4 + nb, 0:390], [PB[4 + b_] for b_ in range(nb)], [t_stage])
                    S.op("dve", lambda: nc.vector.memset(psum[:, 4:7, 0:390], 0.0), writes=[PB[4], PB[5], PB[6]])
                    sts = [next_stat() for _ in range(nq)]
                    if diff:
                        for jq in range(nq):
                            st, t_st = sts[jq]
                            recip(st[:, 0:1], st_ap(jq)[:, 128:129], [t_stage], [t_st])
                            recip(st[:, 1:2], st_ap(nq + jq)[:, 128:129], [t_stage], [t_st])
                        for jq in range(nq):
                            st, t_st = sts[jq]
                            tt("dve", st[:, 1:2], st[:, 1:2], nlam, ALU.mult, [t_st, t_lst], [t_st])
                        for jq in range(nq):
                            st, t_st = sts[jq]
                            ts("dve", o32[jq][0], st_ap(jq)[:, 0:128], st[:, 0:1], ALU.mult, [t_stage, t_st], [o32[jq][1]])
                        for jq in range(nq):
                            st, t_st = sts[jq]
                            stt("dve", o32[jq][0], st_ap(nq + jq)[:, 0:128], st[:, 1:2], o32[jq][0], ALU.mult, ALU.add,
                                [t_stage, t_st, o32[jq][1]], [o32[jq][1]])

                        def stage2():
                            for jq in range(nq):
                                st, t_st = sts[jq]
                                act(junk[:, jq * 128:(jq + 1) * 128], o32[jq][0], AF.Square, [o32[jq][1]], [t_junk, t_st],
                                    accum=st[:, 2:3])
                            for jq in range(nq):
                                st, t_st = sts[jq]
                                act(st[:, 3:4], st[:, 2:3], AF.Ln, [t_st], [t_st], scale=1.0 / 128, bias=EPS)
                            for jq in range(nq):
                                st, t_st = sts[jq]
                                act(st[:, 3:4], st[:, 3:4], AF.Exp, [t_st], [t_st], scale=-0.5)

                        def stage3():
                            for jq in range(nq):
                                st, t_st = sts[jq]
                                stt("dve", obf[jq][0], o32[jq][0], st[:, 3:4], gsub, ALU.mult, ALU.mult,
                                    [o32[jq][1], t_st, t_gsub], [obf[jq][1]])
                    else:
                        for jq in range(nq):
                            st, t_st = sts[jq]
                            recip(st[:, 0:1], st_ap(jq)[:, 128:129], [t_stage], [t_st])
                        for jq in range(nq):
                            st, t_st = sts[jq]
                            ts("dve", obf[jq][0], st_ap(jq)[:, 0:128], st[:, 0:1], ALU.mult, [t_stage, t_st], [obf[jq][1]])

                        def stage2():
                            pass

                        def stage3():
                            pass

                    def stage4():
                        pb7 = bank_bf(7)
                        for jq in range(nq):
                            tr(pb7[:, jq * 128:(jq + 1) * 128], obf[jq][0], [obf[jq][1]], [PB[7]])

                    def stage5():
                        pb7 = bank_bf(7)
                        ot, t_ot = OTst[cnt["st"] % 2]
                        cnt["st"] += 1
                        copy("act", ot[:, 0:Wq], pb7[:, 0:Wq], [PB[7]], [t_ot])
                        dma("pool", oT_d[h, :, q0:q0 + Wq], ot[:, 0:Wq], [t_ot], [t_oTd])
                    return [[3, stage2], [5, stage3], [7, stage4], [9, stage5]]

                deferred = []
                S.op("dve", lambda: nc.vector.memset(psum[:, 4:7, 0:390], 0.0), writes=[PB[4], PB[5], PB[6]])
                emit_qk_exp(allsteps[0])
                for n_, stp in enumerate(allsteps):
                    if n_ + 1 < len(allsteps):
                        emit_qk_exp(allsteps[n_ + 1])
                    emit_pv(stp)
                    for d_ in deferred:
                        d_[0] -= 1
                    while deferred and (deferred[0][0] <= 0 or stp["last"]):
                        deferred.pop(0)[1]()
                    if stp["last"]:
                        deferred = emit_evac(stp)
                while deferred:
                    deferred.pop(0)[1]()
            S.barrier()
            AR.release(m0)

        def phase_post(i, wo_src, last, bias_src=None):
            m0 = AR.mark()
            wo, t_wo = AR.alloc([128, 8, D], BF16)
            wv = wo_src.rearrange("(k p) n -> p k n", p=128)
            for k in range(0, 8, 4):
                dma("pool", wo[:, k:k + 4, :], wv[:, k:k + 4, :], [], [t_wo])
            nog = 3 if bias_src is None else 2
            OTg = [AR.alloc([128, 8, 256], BF16) for _ in range(nog)]
            xg = [AR.alloc([128, 2, D], F32) for _ in range(3)]
            t32s = [AR.alloc([128, D], F32) for _ in range(2)]
            if bias_src is not None:
                gbo, t_gbo = AR.alloc([128, D], F32)
            groups = [(2 * g_, 2) for g_ in range(NT // 2)]
            if last:
                groups = [g_ for g_ in groups if g_[0] < 32]

            def load_og(idx):
                if idx >= len(groups):
                    return
                t0, gs = groups[idx]
                og, t_og = OTg[idx % nog]
                c0 = t0 * 128
                dma("sp", og[:, :, 0:gs * 128], oT_d[:, :, c0:c0 + gs * 128].rearrange("h p t -> p h t"), [t_oTd], [t_og])

            def load_x(idx):
                if idx >= len(groups):
                    return
                t0, gs = groups[idx]
                xgb, t_xg = xg[idx % 3]
                dma("sp", xgb[:, 0:gs, :], xsrc(i, t0, gs), t_xs[t0:t0 + gs], [t_xg])

            for k_ in range(nog - 1):
                load_og(k_)
            load_x(0)
            load_x(1)
            ti = 0
            for idx, (t0, gs) in enumerate(groups):
                isctx = t0 >= 32
                if idx == 0 or isctx:
                    row = 1 if isctx else 0
                    dma("sp", gb, modsd[row, 2 * D:3 * D].partition_broadcast(128), [t_modsd], [t_gb])
                    if bias_src is not None:
                        dma("sp", gbo, bias_src.partition_broadcast(128), [], [t_gbo])
                        tt("dve", gbo, gbo, gb, ALU.mult, [t_gbo, t_gb], [t_gbo])
                load_og(idx + nog - 1)
                load_x(idx + 2)
                Wd = gs * 128
                c0 = t0 * 128
                og, t_og = OTg[idx % nog]
                xgb, t_xg = xg[idx % 3]
                for jj in range(gs):
                    b0 = 2 * (ti % 2)
                    t32, t_t32 = t32s[ti % 2]
                    ti += 1
                    for n in range(2):
                        for h in range(8):
                            mm(psum[:, b0 + n, :], og[:, h, jj * 128:(jj + 1) * 128], wo[:, h, n * 512:(n + 1) * 512],
                               h == 0, h == 7, [t_og, t_wo], [PB[b0 + n]])
                    tt("dve", t32, psum[:, b0:b0 + 2, :].rearrange("p a b -> p (a b)"), gb, ALU.mult,
                       [PB[b0], PB[b0 + 1], t_gb], [t_t32])
                    if bias_src is not None:
                        tt("dve", t32, t32, gbo, ALU.add, [t_t32, t_gbo], [t_t32])
                    tt("pool", xgb[:, jj, :], xgb[:, jj, :], t32, ALU.add, [t_xg, t_t32], [t_xg])
                dma("pool", xs[c0:c0 + Wd, :].rearrange("(j p) d -> p j d", p=128), xgb[:, 0:gs, :], [t_xg], t_xs[t0:t0 + gs])
            S.barrier()
            AR.release(m0)

        def phase_ffn(i, last):
            m0 = AR.mark()
            wgu, t_wgu = ffn_w["gu"]
            wdn, t_wdn = ffn_w["dn"]
            xn = [AR.alloc([128, D], F32) for _ in range(2)]
            hbf4 = [AR.alloc([128, D], BF16) for _ in range(4)]
            xr = [AR.alloc([128, D], F32) for _ in range(1 if last else 2)]
            t32, t_t32 = AR.alloc([128, D], F32)
            hT, t_hT = AR.alloc([128, 8, 512], BF16)
            actT, t_actT = AR.alloc([128, 22, 512], BF16)
            sg = [AR.alloc([128, 512], F32)]
            if last:
                fnb, t_fnb = AR.alloc([128, D], F32)
                dma("sp", fnb, final_norm.partition_broadcast(128), [], [t_fnb])
            groups = [g_ for g_ in GROUPS if not (last and g_[0] >= 32)]
            cnt = dict(xn=0, ti=0, ui=0, di=0)
            xn_of = {}

            def xs_tile(t):
                return xs[t * 128:(t + 1) * 128, :]

            def loadn(gidx, jj):
                t0, gs = groups[gidx]
                buf = xn[cnt["xn"] % 2]
                cnt["xn"] += 1
                xn_of[(gidx, jj)] = buf
                dma("sp", buf[0], xs_tile(t0 + jj), [t_xs[t0 + jj]], [buf[1]])

            def compA(gidx, jj):
                x_ap, t_x = xn_of[(gidx, jj)]
                norm_mod_A(x_ap, t_x, t32, t_t32, hbf4[jj][0], hbf4[jj][1])

            def partB(gidx):
                t0, gs = groups[gidx]
                for jj in range(gs):
                    norm_mod_B(hbf4[jj][0], hbf4[jj][1], cnt["ti"] % 2, hT[:, :, jj * 128:(jj + 1) * 128], t_hT)
                    cnt["ti"] += 1

            prep_mod(i, 2, 0, (t32, t_t32))
            gs0 = groups[0][1]
            loadn(0, 0)
            loadn(0, 1)
            for jj in range(gs0):
                compA(0, jj)
                if jj + 2 < gs0:
                    loadn(0, jj + 2)
            partB(0)
            for gidx, (t0, gs) in enumerate(groups):
                Wd = gs * 128
                nxt = gidx + 1 < len(groups)
                sched = {}
                if nxt:
                    t1, gs1 = groups[gidx + 1]
                    nctx = t1 >= 32

                    def s0(gidx=gidx, nctx=nctx):
                        if nctx:
                            prep_mod(i, 2, 1, (t32, t_t32), need_g=False)
                        loadn(gidx + 1, 0)
                        loadn(gidx + 1, 1)
                    sched[0] = s0
                    for q_, m_ in enumerate((4, 8, 12, 16)):
                        if q_ < gs1:
                            def sA(gidx=gidx, q_=q_, gs1=gs1):
                                compA(gidx + 1, q_)
                                if q_ + 2 < gs1:
                                    loadn(gidx + 1, q_ + 2)
                            sched[m_] = sA
                nxr = len(xr)

                def loadr(jj, t0=t0):
                    buf = xr[jj % nxr]
                    dma("sp", buf[0], xs_tile(t0 + jj), [t_xs[t0 + jj]], [buf[1]])

                for m in range(22):
                    if m in sched:
                        sched[m]()
                    if m == 10:
                        for jj in range(min(nxr, gs)):
                            loadr(jj)
                    bg = 2 + 2 * (cnt["ui"] % 2)
                    cnt["ui"] += 1
                    for k in range(8):
                        mm(psum[:, bg, 0:Wd], wgu[:, k, m * 128:(m + 1) * 128], hT[:, k, 0:Wd], k == 0, k == 7,
                           [t_wgu, t_hT], [PB[bg]])
                    for k in range(8):
                        mm(psum[:, bg + 1, 0:Wd], wgu[:, k, DFF + m * 128:DFF + (m + 1) * 128], hT[:, k, 0:Wd], k == 0, k == 7,
                           [t_wgu, t_hT], [PB[bg + 1]])
                    s_, t_s = sg[0]
                    act(s_[:, 0:Wd], psum[:, bg, 0:Wd], AF.Silu, [PB[bg]], [t_s])
                    tt("dve", actT[:, m, 0:Wd], psum[:, bg + 1, 0:Wd], s_[:, 0:Wd], ALU.mult, [PB[bg + 1], t_s], [t_actT])
                for jj in range(gs):
                    xr_ap, t_xr = xr[jj % nxr]
                    for n in range(2):
                        bd = 6 + (cnt["di"] % 2)
                        cnt["di"] += 1
                        for m in range(22):
                            mm(psum[:, bd, :], actT[:, m, jj * 128:(jj + 1) * 128], wdn[:, m, n * 512:(n + 1) * 512],
                               m == 0, m == 21, [t_actT, t_wdn], [PB[bd]])
                        tt("dve", t32[:, n * 512:(n + 1) * 512], psum[:, bd, :], gb[:, n * 512:(n + 1) * 512], ALU.mult,
                           [PB[bd], t_gb], [t_t32])
                    tt("pool", xr_ap, xr_ap, t32, ALU.add, [t_xr, t_t32], [t_xr])
                    if last:
                        r, t_r = rms_rstd(xr_ap, t_xr, D)
                        stt("dve", xr_ap, xr_ap, r, fnb, ALU.mult, ALU.mult, [t_xr, t_r, t_fnb], [t_xr])
                    dst = out if last else xs
                    dma("pool", dst[(t0 + jj) * 128:(t0 + jj + 1) * 128, :], xr_ap, [t_xr],
                        [T()] if last else [t_xs[t0 + jj]])
                    if jj + nxr < gs:
                        loadr(jj + nxr)
                    if nxt and jj == min(1, gs - 1):
                        partB(gidx + 1)
                if nxt and groups[gidx + 1][0] >= 32:
                    dma("sp", gb, modsd[1, 5 * D:6 * D].partition_broadcast(128), [t_modsd], [t_gb])
            S.barrier()
            AR.release(m0)

        def phase_fnet(i, last):
            m0 = AR.mark()
            tabc, t_tabc = AR.alloc([128, 3, 2, 256], BF16)
            dma("sp", tabc, dft256_in.rearrange("a p k n -> p a k n"), [], [t_tabc])
            xg = [AR.alloc([128, 4, D], F32) for _ in range(2)]
            t32, t_t32 = AR.alloc([128, D], F32)
            hbfs = [AR.alloc([128, D], BF16) for _ in range(2)]
            hTs = [AR.alloc([128, 8, 512], BF16) for _ in range(2)]
            Gst = [AR.alloc([128, 4, 2048], BF16) for _ in range(2)]
            ti = 0
            f1groups = [g_ for g_ in GROUPS if not (last and g_[0] >= 32)]

            def f1load(gi):
                t0, gs = f1groups[gi]
                xgb, t_xg = xg[gi % 2]
                dma("sp", xgb[:, 0:gs, :], xsrc(i, t0, gs), t_xs[t0:t0 + gs], [t_xg])

            f1load(0)
            for gi, (t0, gs) in enumerate(f1groups):
                isctx = t0 >= 32
                if gi + 1 < len(f1groups):
                    f1load(gi + 1)
                if gi == 0 or isctx:
                    prep_mod(i, 1, 1 if isctx else 0, (t32, t_t32), need_g=False)
                Wd = gs * 128
                c0 = t0 * 128
                xgb, t_xg = xg[gi % 2]
                hT, t_hT = hTs[gi % 2]
                gst, t_gst = Gst[gi % 2]
                for jj in range(gs):
                    hbf, t_hbf = hbfs[ti % 2]
                    norm_mod_T(xgb[:, jj, :], t_xg, t32, t_t32, hbf, t_hbf, ti % 2, hT[:, :, jj * 128:(jj + 1) * 128], t_hT)
                    ti += 1
                for jj in range(gs):
                    for part in range(2):
                        for half in range(2):
                            b = 2 + part * 2 + half
                            for gg in range(2):
                                g = half * 2 + gg
                                for kk in range(2):
                                    mm(psum[:, b, gg * 256:(gg + 1) * 256], hT[:, 2 * g + kk, jj * 128:(jj + 1) * 128],
                                       tabc[:, part, kk, :], kk == 0, kk == 1, [t_hT, t_tabc], [PB[b]])
                    copy("act", gst[:, jj, :], psum[:, 2:6, :].rearrange("p a b -> p (a b)"),
                         [PB[2], PB[3], PB[4], PB[5]], [t_gst])
                dma("pool", g_d[c0:c0 + Wd, :].rearrange("(j p) n -> p j n", p=128), gst[:, 0:gs, :], [t_gst], [t_gd])
            S.barrier()
            AR.release(m0)
            m0 = AR.mark()
            tabc, t_tabc = AR.alloc([128, 3, 2, 256], BF16)
            dma("sp", tabc, dft256_in.rearrange("a p k n -> p a k n"), [], [t_tabc])
            Gh, t_Gh = AR.alloc([128, 32, 2, 512], BF16)
            Tb = [AR.alloc([128, 32, 2, 256], BF16) for _ in range(2)]
            fst = [AR.alloc([128, 4, 256], BF16) for _ in range(2)]
            gv = g_d[0:L, :].rearrange("(tc p) (ri c) -> p tc ri c", p=128, ri=2)
            dv = dftL_in.rearrange("cs (tc p) o -> p tc cs o", p=128)
            bi = 0
            tbi = 0

            def tload(k_):
                oc_ = k_ % 16
                tb_, t_tb = Tb[k_ % 2]
                for q2 in range(2):
                    for cs in range(2):
                        dma("sp", tb_[:, q2 * 16:(q2 + 1) * 16, cs, :], dv[:, q2 * 16:(q2 + 1) * 16, cs, oc_ * 256:(oc_ + 1) * 256],
                            [], [t_tb])

            tload(0)
            for half in range(2):
                for q4 in range(4):
                    for ri in range(2):
                        dma("sp", Gh[:, q4 * 8:(q4 + 1) * 8, ri, :], gv[:, q4 * 8:(q4 + 1) * 8, ri, half * 512:(half + 1) * 512],
                            [t_gd], [t_Gh])
                for oc in range(16):
                    tb_, t_tb = Tb[tbi % 2]
                    f_, t_f = fst[tbi % 2]
                    tbi += 1
                    if tbi < 32:
                        tload(tbi)
                    for m in range(4):
                        b = bi % 4
                        bi += 1
                        for tc in range(32):
                            for part in range(2):
                                mm(psum[:, b, 0:256], Gh[:, tc, part, m * 128:(m + 1) * 128], tb_[:, tc, part, :],
                                   tc == 0 and part == 0, tc == 31 and part == 1, [t_Gh, t_tb], [PB[b]])
                        copy("act" if m % 2 == 0 else "dve", f_[:, m, :], psum[:, b, 0:256], [PB[b]], [t_f])
                    dma("pool", oT_d[half * 4:(half + 1) * 4, :, oc * 256:(oc + 1) * 256].rearrange("h p t -> p h t"), f_,
                        [t_f], [t_oTd])
            if not last:
                Gc, t_Gc = AR.alloc([128, 2, 2, D], BF16)
                fc, t_fc = AR.alloc([128, 8, 256], BF16)
                dma("sp", Gc, g_d[L:NTOK, :].rearrange("(tc p) (ri c) -> p tc ri c", p=128, ri=2), [t_gd], [t_Gc])
                for m in range(8):
                    b = bi % 4
                    bi += 1
                    for tc in range(2):
                        for part in range(2):
                            tabsel = 0 if part == 0 else 2
                            mm(psum[:, b, 0:256], Gc[:, tc, part, m * 128:(m + 1) * 128], tabc[:, tabsel, tc, :],
                               tc == 0 and part == 0, tc == 1 and part == 1, [t_Gc, t_tabc], [PB[b]])
                    copy("act" if m % 2 == 0 else "dve", fc[:, m, :], psum[:, b, 0:256], [PB[b]], [t_fc])
                dma("sp", oT_d[:, :, L:NTOK].rearrange("h p t -> p h t"), fc, [t_fc], [t_oTd])
            S.barrier()
            AR.release(m0)

        for i in layers:
            last = i == DEPTH - 1
            kind, j = i % 3, i // 3
            S.new_epoch()
            phase_mod(i)
            if kind in (0, 1):
                phase_qkv(i, kind, j, last)
                mL = AR.mark()
                load_ffn_weights(i)
                phase_att(i, kind, j, last)
                S.new_epoch()
                wo_src = a_w_o[j] if kind == 0 else b_w_o[j]
                phase_post(i, wo_src, last)
            else:
                phase_fnet(i, last)
                mL = AR.mark()
                load_ffn_weights(i)
                S.new_epoch()
                phase_post(i, c_w_o[j], last, bias_src=c_b_o[j, :])
            phase_ffn(i, last)
            AR.release(mL)
        stats = S.emit()
    return nc, stats


_CONST = {}


def _consts():
    if _CONST:
        return _CONST
    bf = ml_dtypes.bfloat16
    _CONST["ident"] = np.eye(128, dtype=np.float32).astype(bf)
    t = np.arange(L)
    rows = (t // 64).astype(np.float32)
    cols = (t % 64).astype(np.float32)

    def rope(head_dim):
        axis_dim = head_dim // 2
        inv = (10000.0 ** (-np.arange(0, axis_dim, 2, dtype=np.float32) / axis_dim)).astype(np.float32)
        ang = np.concatenate([rows[:, None] * inv, cols[:, None] * inv], axis=-1).astype(np.float32)
        cs = np.stack([np.cos(ang), np.sin(ang)]).astype(np.float32)
        return np.ascontiguousarray(cs.reshape(2, 32, 128, -1).transpose(0, 2, 1, 3))

    _CONST["ropeA"] = rope(64)
    _CONST["ropeB"] = rope(128)
    k = np.arange(L, dtype=np.int64)
    m = (k[:, None] * k[None, :]) % L
    ang = (2.0 * np.pi / L) * m.astype(np.float64)
    dl = np.empty((2, L, L), dtype=bf)
    dl[0] = (np.cos(ang) / 64.0).astype(np.float32).astype(bf)
    dl[1] = (np.sin(ang) / 64.0).astype(np.float32).astype(bf)
    _CONST["dftL"] = dl
    k2 = np.arange(256, dtype=np.int64)
    a2 = (2.0 * np.pi / 256) * ((k2[:, None] * k2[None, :]) % 256).astype(np.float64)
    c2 = (np.cos(a2) / 16.0)
    s2 = (np.sin(a2) / 16.0)
    tab = np.stack([c2, -s2, s2]).astype(np.float32).astype(bf)
    _CONST["dft256"] = np.ascontiguousarray(tab.reshape(3, 2, 128, 256).transpose(0, 2, 1, 3))
    return _CONST


_NC_CACHE = {}


def _get_nc(layers=(0, 1, 2, 3)):
    key = tuple(layers)
    if key not in _NC_CACHE:
        _NC_CACHE[key] = build_nc(layers)
    return _NC_CACHE[key]


def make_in_maps(inputs):
    cst = _consts()
    f = lambda a: np.ascontiguousarray(np.asarray(a, dtype=np.float32))
    shared = {
        "mod_w": f(inputs["mod_w"]), "mod_b": f(inputs["mod_b"]),
        "ln_mix": f(inputs["ln_mix"]), "ln_ffn": f(inputs["ln_ffn"]),
        "ffn_w_gu": f(inputs["ffn_w_gu"]), "ffn_w_down": f(inputs["ffn_w_down"]),
        "a_w_qkv": f(inputs["a_w_qkv"]), "a_lam": f(inputs["a_lam"]).reshape(2, 256),
        "a_subln": f(inputs["a_subln"]), "a_w_o": f(inputs["a_w_o"]),
        "b_w_qkv": f(inputs["b_w_qkv"]), "b_q_norm": f(inputs["b_q_norm"]), "b_k_norm": f(inputs["b_k_norm"]),
        "b_w_o": f(inputs["b_w_o"]), "c_w_o": f(inputs["c_w_o"]), "c_b_o": f(inputs["c_b_o"]),
        "final_norm": f(inputs["final_norm"]),
        "ident": cst["ident"], "ropeA": cst["ropeA"], "ropeB": cst["ropeB"],
        "dftL": cst["dftL"], "dft256": cst["dft256"],
    }
    x = f(inputs["x"])
    ctx = f(inputs["ctx"])
    c = f(inputs["c"])
    cc = f(inputs["c_ctx"])
    maps = []
    for b in range(8):
        ccol = np.ascontiguousarray(np.stack([c[b].reshape(8, 128).T, cc.reshape(8, 128).T], axis=-1))
        d = dict(shared)
        d["x"] = x[b]
        d["ctx"] = ctx[b]
        d["ccol"] = ccol
        maps.append(d)
    return maps


def kernel(**inputs):
    nc, _ = _get_nc()
    maps = make_in_maps(inputs)
    res = run_bass_kernel_spmd(nc, maps, core_ids=list(range(8)))
    return np.stack([np.asarray(r["out"], dtype=np.float32) for r in res.results], axis=0)
```

```python
import numpy as np
import ml_dtypes
from contextlib import ExitStack
import concourse.bass as bass
import concourse.mybir as mybir
from concourse.bass_utils import run_bass_kernel_spmd

F32 = mybir.dt.float32
BF16 = mybir.dt.bfloat16
AF = mybir.ActivationFunctionType
ALU = mybir.AluOpType
AX = mybir.AxisListType

D = 1024
L = 4096
C = 256
NTOK = L + C
NT = NTOK // 128
DFF = 2816
DEPTH = 4
EPS = 1e-6
GROUPS = [(g * 4, 4) for g in range(8)] + [(32, 2)]
NQKV_A = 3072
NQKV_B = 1536


def lambda_init_fn(i):
    return 0.8 - 0.6 * float(np.exp(-0.3 * i))


class T:
    __slots__ = ("name", "w", "re", "rd")

    def __init__(self, name=""):
        self.name = name
        self.w = None
        self.re = {}
        self.rd = []


class Sched:
    ENG = ("pe", "act", "dve", "pool", "sp")

    def __init__(self, nc, es, n_epochs=8, n_dma_sems=12):
        self.nc = nc
        self.eng = {"pe": nc.tensor, "act": nc.scalar, "dve": nc.vector,
                    "pool": nc.gpsimd, "sp": nc.sync}
        self.ops = []
        self.epoch = 0
        self.n_epochs = n_epochs
        self.csem = {}
        for e in ("pe", "act", "dve", "pool"):
            for ep in range(n_epochs):
                self.csem[(e, ep)] = es.enter_context(nc.semaphore(f"c_{e}_{ep}"))
        self.dsem = {}
        for q in ("sp", "pool"):
            self.dsem[q] = [es.enter_context(nc.semaphore(f"d_{q}_{i}")) for i in range(n_dma_sems)]
        self.last_on = {}
        self.dma_since = []
        self.pending_barrier = {}

    def new_epoch(self):
        if self.epoch + 1 < self.n_epochs:
            self.epoch += 1

    def op(self, engine, fn, reads=(), writes=(), dma=False):
        idx = len(self.ops)
        deps = set()
        for t in reads:
            if t.w is not None:
                deps.add(t.w)
        for t in writes:
            if t.w is not None:
                deps.add(t.w)
            deps.update(t.re.values())
            deps.update(t.rd)
        if engine in self.pending_barrier:
            deps.update(self.pending_barrier.pop(engine))
        for t in writes:
            t.w = idx
            t.re = {}
            t.rd = []
        for t in reads:
            if t.w == idx:
                continue
            if dma:
                t.rd.append(idx)
            else:
                t.re[engine] = idx
        deps.discard(idx)
        self.ops.append([engine, fn, deps, dma, False, self.epoch])
        if dma:
            self.dma_since.append(idx)
        else:
            self.last_on[engine] = idx
        return idx

    def barrier(self):
        deps = set(self.dma_since)
        deps.update(self.last_on.values())
        self.dma_since = []
        for e in self.ENG:
            s = self.pending_barrier.get(e, set())
            s.update(deps)
            self.pending_barrier[e] = s

    def emit(self, final_wait_engine="sp"):
        ops = self.ops
        self.barrier()
        fin = self.pending_barrier[final_wait_engine]
        ops.append([final_wait_engine, None, set(fin), False, False, self.epoch])
        for engine, fn, deps, dma, _, ep in ops:
            for d in deps:
                p = ops[d]
                if p[3]:
                    continue
                if p[0] == "pe" and engine == "pe" and not dma:
                    continue
                p[4] = True
        ccount = {k: 0 for k in self.csem}
        incval = {}
        dma_n = {"sp": 0, "pool": 0}
        dma_cnt = {}
        dma_info = {}
        waited = {e: {} for e in self.ENG}
        nwaits = 0
        for idx, (engine, fn, deps, dma, need_inc, ep) in enumerate(ops):
            E = self.eng[engine]
            w = waited[engine]
            need = {}
            for d in deps:
                p = ops[d]
                if p[3]:
                    sem, val = dma_info[d]
                else:
                    if p[0] == "pe" and engine == "pe" and not dma:
                        continue
                    sem, val = self.csem[(p[0], p[5])], incval[d]
                k = id(sem)
                if w.get(k, 0) < val and need.get(k, (None, 0))[1] < val:
                    need[k] = (sem, val)
            if dma:
                pool = self.dsem[engine]
                sidx = dma_n[engine] % len(pool)
                dma_n[engine] += 1
                sem = pool[sidx]
                k = id(sem)
                prev = dma_cnt.get(k, 0)
                if prev > 0 and w.get(k, 0) < prev and need.get(k, (None, 0))[1] < prev:
                    need[k] = (sem, prev)
                dma_cnt[k] = prev + 16
                dma_info[idx] = (sem, prev + 16)
            for k, (sem, val) in need.items():
                E.wait_ge(sem, val)
                w[k] = val
                nwaits += 1
            if fn is None:
                continue
            inst = fn()
            if dma:
                inst.then_inc(dma_info[idx][0], 16)
            elif need_inc:
                key = (engine, ep)
                ccount[key] += 1
                incval[idx] = ccount[key]
                inst.then_inc(self.csem[key], 1)
        self.stats = dict(n_ops=len(ops), n_waits=nwaits, counts={k: v for k, v in ccount.items() if v},
                          dma=dict(dma_n))
        return self.stats


class Arena:
    def __init__(self, ap_bf16, nbytes):
        self.ap = ap_bf16
        self.n = nbytes
        self.off = 0

    def alloc(self, shape, dt):
        esz = 4 if dt == F32 else 2
        free = int(np.prod(shape[1:]))
        nb = free * esz
        nb = (nb + 63) // 64 * 64
        assert self.off + nb <= self.n, (self.off, nb, self.n)
        v = self.ap[:, self.off // 2:(self.off + free * esz) // 2]
        if dt == F32:
            v = v.bitcast(F32)
        if len(shape) == 3:
            v = v.rearrange("p (a b) -> p a b", a=shape[1])
        elif len(shape) == 4:
            v = v.rearrange("p (a b c) -> p a b c", a=shape[1], b=shape[2])
        self.off += nb
        return v, T()

    def mark(self):
        return self.off

    def release(self, m):
        self.off = m


def build_nc(layers=(0, 1, 2, 3), dbg=False):
    nc = bass.Bass("TRN2", target_bir_lowering=False)

    def din(name, shape, dt=F32):
        return nc.dram_tensor(name, list(shape), dt, kind="ExternalInput").ap()

    def dscr(name, shape, dt):
        return nc.dram_tensor(name, list(shape), dt, kind="Internal").ap()

    x_in = din("x", [L, D])
    ctx_in = din("ctx", [C, D])
    ccol_in = din("ccol", [128, 8, 2])
    mod_w = din("mod_w", [DEPTH, D, 6 * D])
    mod_b = din("mod_b", [DEPTH, 6 * D])
    ln_mix = din("ln_mix", [DEPTH, D])
    ln_ffn = din("ln_ffn", [DEPTH, D])
    ffn_w_gu = din("ffn_w_gu", [DEPTH, D, 2 * DFF])
    ffn_w_down = din("ffn_w_down", [DEPTH, DFF, D])
    a_w_qkv = din("a_w_qkv", [2, D, NQKV_A])
    a_lam = din("a_lam", [2, 256])
    a_subln = din("a_subln", [2, 128])
    a_w_o = din("a_w_o", [2, D, D])
    b_w_qkv = din("b_w_qkv", [1, D, NQKV_B])
    b_q_norm = din("b_q_norm", [1, 128])
    b_k_norm = din("b_k_norm", [1, 128])
    b_w_o = din("b_w_o", [1, D, D])
    c_w_o = din("c_w_o", [1, D, D])
    c_b_o = din("c_b_o", [1, D])
    final_norm = din("final_norm", [D])
    ident_in = din("ident", [128, 128], BF16)
    ropeA_in = din("ropeA", [2, 128, 32, 32])
    ropeB_in = din("ropeB", [2, 128, 32, 64])
    dftL_in = din("dftL", [2, L, L], BF16)
    dft256_in = din("dft256", [3, 128, 2, 256], BF16)
    out = nc.dram_tensor("out", [L, D], F32, kind="ExternalOutput").ap()

    xs = dscr("xs", [NTOK, D], F32)
    modsd = dscr("modsd", [2, 6 * D], F32)
    qT_d = dscr("qT_d", [8, 128, NTOK], BF16)
    kT_d = dscr("kT_d", [8, 128, NTOK], BF16)
    v_d = dscr("v_d", [128, NT, 8, 130], BF16)
    oT_d = dscr("oT_d", [8, 128, NTOK], BF16)
    g_d = dscr("g_d", [NTOK, 2048], BF16)

    es = ExitStack()
    with es:
        S = Sched(nc, es, n_epochs=9, n_dma_sems=12)
        ARENA_BYTES = 212000
        arena_t = es.enter_context(nc.sbuf_tensor("arena", [128, ARENA_BYTES // 2], BF16))
        AR = Arena(arena_t, ARENA_BYTES)
        psum = es.enter_context(nc.psum_tensor("psum", [128, 8, 512], F32))
        PB = [T(f"bank{i}") for i in range(8)]

        def bank_bf(i):
            return psum[:, i, :].bitcast(BF16)

        def mm(out_ap, lhsT, rhs, start, stop, R, W, skip=False):
            if skip:
                S.op("pe", lambda: nc.tensor.matmul(out_ap, lhsT=lhsT, rhs=rhs, start=start, stop=stop,
                                                    skip_group_check=True), reads=R, writes=W)
            else:
                S.op("pe", lambda: nc.tensor.matmul(out_ap, lhsT=lhsT, rhs=rhs, start=start, stop=stop),
                     reads=R, writes=W)

        def tr(out_ap, in_ap, R, W):
            S.op("pe", lambda: nc.tensor.transpose(out_ap, in_ap, ident[:]), reads=list(R) + [t_ident], writes=W)

        def dma(q, out_ap, in_ap, R, W):
            e = nc.sync if q == "sp" else nc.gpsimd
            S.op(q, lambda: e.dma_start(out=out_ap, in_=in_ap), reads=R, writes=W, dma=True)

        def act(out_ap, in_ap, func, R, W, scale=1.0, bias=0.0, accum=None):
            if accum is None:
                S.op("act", lambda: nc.scalar.activation(out=out_ap, in_=in_ap, func=func, scale=scale, bias=bias),
                     reads=R, writes=W)
            else:
                S.op("act", lambda: nc.scalar.activation(out=out_ap, in_=in_ap, func=func, scale=scale, bias=bias,
                                                         accum_out=accum), reads=R, writes=W)

        def tt(eng, out_ap, a, b, op, R, W):
            e = nc.vector if eng == "dve" else nc.gpsimd
            S.op(eng, lambda: e.tensor_tensor(out=out_ap, in0=a, in1=b, op=op), reads=R, writes=W)

        def ts(eng, out_ap, a, s1, op0, R, W, s2=None, op1=None):
            e = nc.vector if eng == "dve" else nc.gpsimd
            if op1 is None:
                S.op(eng, lambda: e.tensor_scalar(out=out_ap, in0=a, scalar1=s1, scalar2=None, op0=op0), reads=R, writes=W)
            else:
                S.op(eng, lambda: e.tensor_scalar(out=out_ap, in0=a, scalar1=s1, scalar2=s2, op0=op0, op1=op1),
                     reads=R, writes=W)

        def stt(eng, out_ap, a, sc, b, op0, op1, R, W):
            e = nc.vector if eng == "dve" else nc.gpsimd
            S.op(eng, lambda: e.scalar_tensor_tensor(out=out_ap, in0=a, scalar=sc, in1=b, op0=op0, op1=op1),
                 reads=R, writes=W)

        def recip(out_ap, in_ap, R, W):
            S.op("dve", lambda: nc.vector.reciprocal(out=out_ap, in_=in_ap), reads=R, writes=W)

        def copy(eng, out_ap, in_ap, R, W):
            if eng == "act":
                S.op("act", lambda: nc.scalar.copy(out=out_ap, in_=in_ap), reads=R, writes=W)
            else:
                e = nc.vector if eng == "dve" else nc.gpsimd
                S.op(eng, lambda: e.tensor_copy(out=out_ap, in_=in_ap), reads=R, writes=W)

        def memset(eng, ap, val, W):
            e = nc.vector if eng == "dve" else nc.gpsimd
            S.op(eng, lambda: e.memset(ap, val), writes=W)

        ident, t_ident = AR.alloc([128, 128], BF16)
        scol, t_scol = AR.alloc([128, 8, 2], F32)
        stat, t_stat = AR.alloc([128, 64], F32)
        Ab, t_Ab = AR.alloc([128, D], F32)
        shb, t_shb = AR.alloc([128, D], F32)
        gb, t_gb = AR.alloc([128, D], F32)
        junk, t_junk = AR.alloc([128, D], BF16)
        BASE = AR.mark()

        dma("sp", ident, ident_in, [], [t_ident])
        dma("sp", scol, ccol_in, [], [t_scol])
        act(scol, scol, AF.Silu, [t_scol], [t_scol])

        stat_slots = [(stat[:, 4 * i:4 * i + 4], T()) for i in range(16)]
        stat_i = [0]

        def next_stat():
            s = stat_slots[stat_i[0] % 16]
            stat_i[0] += 1
            return s

        def xsrc(layer, t0, gs):
            if layer == layers[0] and layer == 0:
                if t0 < 32:
                    a = x_in[t0 * 128:(t0 + gs) * 128, :]
                else:
                    a = ctx_in[(t0 - 32) * 128:(t0 - 32 + gs) * 128, :]
            else:
                a = xs[t0 * 128:(t0 + gs) * 128, :]
            return a.rearrange("(j p) d -> p j d", p=128)

        def prep_mod(i, kind, row, scr, need_g=True):
            ln = ln_mix if kind == 1 else ln_ffn
            o = 0 if kind == 1 else 3
            lnb, t_lnb = scr
            dma("sp", lnb, ln[i, :].partition_broadcast(128), [], [t_lnb])
            dma("sp", Ab, modsd[row, (o + 1) * D:(o + 2) * D].partition_broadcast(128), [t_modsd], [t_Ab])
            dma("sp", shb, modsd[row, o * D:(o + 1) * D].partition_broadcast(128), [t_modsd], [t_shb])
            if need_g:
                dma("sp", gb, modsd[row, (o + 2) * D:(o + 3) * D].partition_broadcast(128), [t_modsd], [t_gb])
            stt("dve", Ab, Ab, 1.0, lnb, ALU.add, ALU.mult, [t_Ab, t_lnb], [t_Ab])

        t_modsd = T("modsd")
        t_xs = [T() for _ in range(NT)]

        def rms_rstd(x_ap, t_x, n):
            st, t_st = next_stat()
            act(junk[:, 0:n], x_ap, AF.Square, [t_x], [t_junk, t_st], accum=st[:, 0:1])
            act(st[:, 1:2], st[:, 0:1], AF.Ln, [t_st], [t_st], scale=1.0 / n, bias=EPS)
            act(st[:, 1:2], st[:, 1:2], AF.Exp, [t_st], [t_st], scale=-0.5)
            return st[:, 1:2], t_st

        def norm_mod_A(x_ap, t_x, t32, t_t32, hbf, t_hbf):
            r, t_r = rms_rstd(x_ap, t_x, D)
            stt("dve", t32, x_ap, r, Ab, ALU.mult, ALU.mult, [t_x, t_r, t_Ab], [t_t32])
            tt("pool", hbf, t32, shb, ALU.add, [t_t32, t_shb], [t_hbf])

        def norm_mod_B(hbf, t_hbf, bank, hT_dst, t_hT):
            pb = bank_bf(bank)
            for k in range(8):
                tr(pb[:, k * 128:(k + 1) * 128], hbf[:, k * 128:(k + 1) * 128], [t_hbf], [PB[bank]])
            copy("act", hT_dst, pb.rearrange("p (k t) -> p k t", k=8), [PB[bank]], [t_hT])

        def norm_mod_T(x_ap, t_x, t32, t_t32, hbf, t_hbf, bank, hT_dst, t_hT):
            r, t_r = rms_rstd(x_ap, t_x, D)
            stt("dve", t32, x_ap, r, Ab, ALU.mult, ALU.mult, [t_x, t_r, t_Ab], [t_t32])
            tt("pool", hbf, t32, shb, ALU.add, [t_t32, t_shb], [t_hbf])
            pb = bank_bf(bank)
            for k in range(8):
                tr(pb[:, k * 128:(k + 1) * 128], hbf[:, k * 128:(k + 1) * 128], [t_hbf], [PB[bank]])
            copy("act", hT_dst, pb.rearrange("p (k t) -> p k t", k=8), [PB[bank]], [t_hT])

        def phase_mod(i):
            m0 = AR.mark()
            wm = [AR.alloc([128, 8, 512], F32) for _ in range(2)]
            mods, t_mods = AR.alloc([128, 6 * D], F32)
            modb, t_modb = AR.alloc([128, 6 * D], F32)
            dma("sp", modb[0:2, :], mod_b[i, :].partition_broadcast(2), [], [t_modb])
            wv = mod_w[i].rearrange("(k p) n -> p k n", p=128)
            for n in range(12):
                buf, t_buf = wm[n % 2]
                dma("sp", buf, wv[:, :, n * 512:(n + 1) * 512], [], [t_buf])
                b = n % 2
                for k in range(8):
                    mm(psum[0:2, b, :], scol[:, k, :], buf[:, k, :], k == 0, k == 7, [t_scol, t_buf], [PB[b]])
                tt("dve", mods[0:2, n * 512:(n + 1) * 512], psum[0:2, b, :], modb[0:2, n * 512:(n + 1) * 512],
                   ALU.add, [PB[b], t_modb], [t_mods])
            dma("sp", modsd, mods[0:2, :], [t_mods], [t_modsd])
            S.barrier()
            AR.release(m0)

        def phase_qkv(i, kind, j, last):
            m0 = AR.mark()
            diff = kind == 0
            nqkv = NQKV_A if diff else NQKV_B
            wsrc = (a_w_qkv if diff else b_w_qkv)[j].rearrange("(k p) n -> p k n", p=128)
            W, t_W = AR.alloc([128, 8, nqkv], BF16)
            for k in range(8):
                dma("pool", W[:, k, :], wsrc[:, k, :], [], [t_W])
            npair = 32 if diff else 64
            rope_src = ropeA_in if diff else ropeB_in
            cosT, t_cos = AR.alloc([128, 32, npair], F32)
            sinT, t_sin = AR.alloc([128, 32, npair], F32)
            dma("sp", cosT, rope_src[0], [], [t_cos])
            dma("sp", sinT, rope_src[1], [], [t_sin])
            nkh = 8 if diff else 2
            xg = [AR.alloc([128, 4, D], F32) for _ in range(2)]
            t32, t_t32 = AR.alloc([128, D], F32)
            hbfs = [AR.alloc([128, D], BF16) for _ in range(2)]
            hTs = [AR.alloc([128, 8, 512], BF16) for _ in range(2)]
            rtmp = [AR.alloc([128, 256], F32) for _ in range(4)]
            qtm = [AR.alloc([128, 512], BF16) for _ in range(2)]
            qst = [AR.alloc([128, 8, 512], BF16) for _ in range(2)]
            kst = [AR.alloc([128, nkh, 512], BF16) for _ in range(2)]
            vst = [AR.alloc([128, 4, nkh, 130], BF16) for _ in range(2)]
            for v_, t_v in vst:
                memset("pool", v_, 1.0, [t_v])
            if not diff:
                sq, t_sq = AR.alloc([128, 512], F32)
                nrm, t_nrm = AR.alloc([128, 512], F32)
                gq, t_gq = AR.alloc([128, 128], F32)
                gk, t_gk = AR.alloc([128, 128], F32)
                dma("sp", gq, b_q_norm[0, :].partition_broadcast(128), [], [t_gq])
                dma("sp", gk, b_k_norm[0, :].partition_broadcast(128), [], [t_gk])
            if diff:
                blocks = [[("q", 0, 512, 0)], [("q", 0, 512, 4)], [("k", 0, 512, 0)], [("k", 0, 512, 4)],
                          [("v", 0, 512, 0)], [("v", 0, 512, 4)]]
            else:
                blocks = [[("q", 0, 512, 0)], [("q", 0, 512, 4)], [("k", 0, 256, 0), ("v", 256, 256, 0)]]
            obank = [2, 3, 4, 5]
            cnt = dict(oi=0, ti=0, rope=0)
            hbf4 = hbfs + [AR.alloc([128, D], BF16) for _ in range(2)]
            live = [g_ for g_ in GROUPS]

            def loadA(gi):
                t0, gs = live[gi]
                xgb, t_xg = xg[gi % 2]
                dma("sp", xgb[:, 0:gs, :], xsrc(i, t0, gs), t_xs[t0:t0 + gs], [t_xg])

            def partA(gi):
                t0, gs = live[gi]
                isctx = t0 >= 32
                if gi == 0 or isctx:
                    prep_mod(i, 1, 1 if isctx else 0, (t32, t_t32), need_g=False)
                xgb, t_xg = xg[gi % 2]
                for jj in range(gs):
                    hbf, t_hbf = hbf4[jj]
                    norm_mod_A(xgb[:, jj, :], t_xg, t32, t_t32, hbf, t_hbf)

            def partB(gi):
                t0, gs = live[gi]
                hT, t_hT = hTs[gi % 2]
                for jj in range(gs):
                    hbf, t_hbf = hbf4[jj]
                    norm_mod_B(hbf, t_hbf, cnt["ti"] % 2, hT[:, :, jj * 128:(jj + 1) * 128], t_hT)
                    cnt["ti"] += 1

            nrm3 = None
            if not diff:
                sq2 = [(sq, t_sq), AR.alloc([128, 512], F32)]
                nrm3 = [(nrm, t_nrm), AR.alloc([128, 512], F32)]
            rsets = [rtmp, [AR.alloc([128, 256], F32) for _ in range(4)]]
            qtm3 = qtm + [AR.alloc([128, 512], BF16)]
            rot = dict(sq=0, nrm=0, rs=0, qtm=0, tb=0)

            def make_entries(gi, jj, n, segs, ob):
                t0, gs = live[gi]
                isctx = t0 >= 32
                tok_tile = t0 + jj
                q_st, t_qst = qst[gi % 2]
                k_st, t_kst = kst[gi % 2]
                v_st, t_vst = vst[gi % 2]
                entries = []
                for (sk, c0, ncol, h0) in segs:
                    psrc = psum[:, ob, c0:c0 + ncol]
                    t_psrc = PB[ob]
                    if sk == "v":
                        def s_v(psrc=psrc, t_psrc=t_psrc, ncol=ncol, h0=h0):
                            nh = ncol // 128
                            copy("act", v_st[:, jj, h0:h0 + nh, 0:128],
                                 psrc.rearrange("p (h d) -> p h d", h=nh), [t_psrc], [t_vst])
                        entries.append([s_v])
                        continue
                    stages = []
                    cur = dict(src=psrc, t=t_psrc)
                    nh = ncol // 128
                    if not diff:
                        g_ap, t_g = (gq, t_gq) if sk == "q" else (gk, t_gk)
                        st, t_st = next_stat()
                        sq_, t_sq_ = sq2[rot["sq"] % 2]
                        rot["sq"] += 1
                        nr_, t_nr_ = nrm3[rot["nrm"] % 2]
                        rot["nrm"] += 1

                        def s_sq(psrc=psrc, t_psrc=t_psrc, ncol=ncol, nh=nh, st=st, t_st=t_st, sq_=sq_, t_sq_=t_sq_):
                            act(sq_[:, 0:ncol], psrc, AF.Square, [t_psrc], [t_sq_])
                            S.op("dve", lambda o=st[:, 0:nh], a_=sq_[:, 0:ncol].rearrange("p (h d) -> p h d", h=nh):
                                 nc.vector.tensor_reduce(out=o, in_=a_, axis=AX.X, op=ALU.add),
                                 reads=[t_sq_], writes=[t_st])
                            act(st[:, 0:nh], st[:, 0:nh], AF.Ln, [t_st], [t_st], scale=1.0 / 128, bias=EPS)
                            act(st[:, 0:nh], st[:, 0:nh], AF.Exp, [t_st], [t_st], scale=-0.5)

                        def s_nrm(psrc=psrc, t_psrc=t_psrc, ncol=ncol, nh=nh, st=st, t_st=t_st, nr_=nr_, t_nr_=t_nr_,
                                  g_ap=g_ap, t_g=t_g):
                            tt("dve", nr_[:, 0:ncol].rearrange("p (h d) -> p h d", h=nh),
                               psrc.rearrange("p (h d) -> p h d", h=nh),
                               st[:, 0:nh].unsqueeze(2).to_broadcast([128, nh, 128]), ALU.mult,
                               [t_psrc, t_st], [t_nr_])
                            tt("pool", nr_[:, 0:ncol].rearrange("p (h d) -> p h d", h=nh),
                               nr_[:, 0:ncol].rearrange("p (h d) -> p h d", h=nh),
                               g_ap[:, :].unsqueeze(1).to_broadcast([128, nh, 128]), ALU.mult,
                               [t_nr_, t_g], [t_nr_])
                        stages += [s_sq, s_nrm]
                        cur = dict(src=nr_[:, 0:ncol], t=t_nr_)
                    q_tm, t_qtm = qtm3[rot["qtm"] % 3]
                    rot["qtm"] += 1
                    src, t_src = cur["src"], cur["t"]
                    if isctx:
                        def s_cp(src=src, t_src=t_src, q_tm=q_tm, t_qtm=t_qtm, ncol=ncol):
                            copy("dve", q_tm[:, 0:ncol], src, [t_src], [t_qtm])
                        stages.append(s_cp)
                    else:
                        rset = rsets[rot["rs"] % 2]
                        rot["rs"] += 1
                        nhr = ncol // (2 * npair)
                        sv = src.rearrange("p (h i two) -> p h i two", h=nhr, two=2)
                        dv = q_tm[:, 0:ncol].rearrange("p (h i two) -> p h i two", h=nhr, two=2)
                        cb = cosT[:, tok_tile, :].unsqueeze(1).to_broadcast([128, nhr, npair])
                        sb_ = sinT[:, tok_tile, :].unsqueeze(1).to_broadcast([128, nhr, npair])
                        rv = [(r_[:, 0:ncol // 2].rearrange("p (h i) -> p h i", h=nhr), t_r) for r_, t_r in rset]

                        def s_rm(sv=sv, cb=cb, sb_=sb_, rv=rv, t_src=t_src):
                            tt("dve", rv[0][0], sv[:, :, :, 0], cb, ALU.mult, [t_src, t_cos], [rv[0][1]])
                            tt("dve", rv[1][0], sv[:, :, :, 1], sb_, ALU.mult, [t_src, t_sin], [rv[1][1]])
                            tt("dve", rv[2][0], sv[:, :, :, 0], sb_, ALU.mult, [t_src, t_sin], [rv[2][1]])
                            tt("dve", rv[3][0], sv[:, :, :, 1], cb, ALU.mult, [t_src, t_cos], [rv[3][1]])

                        def s_ra(dv=dv, rv=rv, t_qtm=t_qtm):
                            tt("pool", dv[:, :, :, 0], rv[0][0], rv[1][0], ALU.subtract, [rv[0][1], rv[1][1]], [t_qtm])
                            tt("pool", dv[:, :, :, 1], rv[2][0], rv[3][0], ALU.add, [rv[2][1], rv[3][1]], [t_qtm])
                        stages += [s_rm, s_ra]
                    tb = 6 + (rot["tb"] % 2)
                    rot["tb"] += 1

                    def s_tr(q_tm=q_tm, t_qtm=t_qtm, nh=nh, tb=tb, sk=sk, h0=h0):
                        pb = bank_bf(tb)
                        for hh in range(nh):
                            tr(pb[:, hh * 128:(hh + 1) * 128], q_tm[:, hh * 128:(hh + 1) * 128], [t_qtm], [PB[tb]])
                        dst, t_dst = (q_st, t_qst) if sk == "q" else (k_st, t_kst)
                        copy("act", dst[:, h0:h0 + nh, jj * 128:(jj + 1) * 128],
                             pb[:, 0:nh * 128].rearrange("p (h t) -> p h t", h=nh), [PB[tb]], [t_dst])
                    stages.append(s_tr)
                    entries.append(stages)
                return entries

            inflight = []

            def pump(drain=False):
                while True:
                    for e_ in list(inflight):
                        e_.pop(0)()
                        if not e_:
                            inflight.remove(e_)
                    if not drain or not inflight:
                        break

            fins = []
            ngr = len(live)
            loadA(0)
            loadA(1)
            partA(0)
            partB(0)
            for gi, (t0, gs) in enumerate(live):
                isctx = t0 >= 32
                Wd = gs * 128
                hT, t_hT = hTs[gi % 2]
                q_st, t_qst = qst[gi % 2]
                k_st, t_kst = kst[gi % 2]
                v_st, t_vst = vst[gi % 2]
                if gi + 1 < ngr:
                    partA(gi + 1)
                for jj in range(gs):
                    for n, segs in enumerate(blocks):
                        if last and isctx and segs[0][0] == "q":
                            continue
                        ob = obank[cnt["oi"] % 4]
                        cnt["oi"] += 1
                        for k in range(8):
                            mm(psum[:, ob, :], hT[:, k, jj * 128:(jj + 1) * 128], W[:, k, n * 512:(n + 1) * 512],
                               k == 0, k == 7, [t_hT, t_W], [PB[ob]])
                        inflight.extend(make_entries(gi, jj, n, segs, ob))
                        pump()
                pump(drain=True)
                if gi + 1 < ngr:
                    partB(gi + 1)
                if gi + 2 < ngr:
                    loadA(gi + 2)
                c0 = t0 * 128
                if not (last and isctx):
                    dma("sp", qT_d[:, :, c0:c0 + Wd].rearrange("h p t -> p h t"), q_st[:, :, 0:Wd], [t_qst], [t_qTd])
                dma("sp", kT_d[0:nkh, :, c0:c0 + Wd].rearrange("h p t -> p h t"), k_st[:, :, 0:Wd], [t_kst], [t_kTd])
                dma("sp", v_d[:, t0:t0 + gs, 0:nkh, :], v_st[:, 0:gs, :, :], [t_vst], [t_vd])
            S.barrier()
            AR.release(m0)

        t_qTd, t_kTd, t_vd, t_oTd, t_gd = T(), T(), T(), T(), T()

        ffn_w = {}

        def load_ffn_weights(i):
            wgu, t_wgu = AR.alloc([128, 8, 2 * DFF], BF16)
            wdn, t_wdn = AR.alloc([128, 22, D], BF16)
            src = ffn_w_gu[i].rearrange("(k p) n -> p k n", p=128)
            for k in range(8):
                dma("pool", wgu[:, k, :], src[:, k, :], [], [t_wgu])
            srcd = ffn_w_down[i].rearrange("(m p) n -> p m n", p=128)
            for m in range(0, 22, 6):
                me = min(22, m + 6)
                dma("pool", wdn[:, m:me, :], srcd[:, m:me, :], [], [t_wdn])
            ffn_w["gu"] = (wgu, t_wgu)
            ffn_w["dn"] = (wdn, t_wdn)

        def phase_att(i, kind, j, last):
            diff = kind == 0
            m0 = AR.mark()
            lam_init = lambda_init_fn(i)
            KT = [AR.alloc([128, NTOK], BF16) for _ in range(2)]
            VA = [AR.alloc([128, NT, 130], BF16) for _ in range(2)]
            QT = [AR.alloc([128, NTOK], BF16)]
            Pb = [AR.alloc([128, 2, 512], BF16) for _ in range(2)]
            o32 = [AR.alloc([128, 128], F32) for _ in range(4)]
            obf = [AR.alloc([128, 128], BF16) for _ in range(4)]
            OTst = [AR.alloc([128, 512], BF16) for _ in range(2)]
            if diff:
                lamt, t_lam = AR.alloc([128, 256], F32)
                gsub, t_gsub = AR.alloc([128, 128], F32)
                lst, t_lst = AR.alloc([128, 8], F32)
                dma("sp", lamt, a_lam[j, :].partition_broadcast(128), [], [t_lam])
                dma("sp", gsub, a_subln[j, :].partition_broadcast(128), [], [t_gsub])
                tt("dve", lamt[:, 0:64], lamt[:, 0:64], lamt[:, 64:128], ALU.mult, [t_lam], [t_lam])
                tt("dve", lamt[:, 128:192], lamt[:, 128:192], lamt[:, 192:256], ALU.mult, [t_lam], [t_lam])
                S.op("dve", lambda: nc.vector.tensor_reduce(out=lst[:, 0:1], in_=lamt[:, 0:64], axis=AX.X, op=ALU.add),
                     reads=[t_lam], writes=[t_lst])
                S.op("dve", lambda: nc.vector.tensor_reduce(out=lst[:, 1:2], in_=lamt[:, 128:192], axis=AX.X, op=ALU.add),
                     reads=[t_lam], writes=[t_lst])
                act(lst[:, 2:4], lst[:, 0:2], AF.Exp, [t_lst], [t_lst])
                tt("dve", lst[:, 4:5], lst[:, 3:4], lst[:, 2:3], ALU.subtract, [t_lst], [t_lst])
                ts("dve", lst[:, 5:6], lst[:, 4:5], -lam_init, ALU.add, [t_lst], [t_lst])
                nlam = lst[:, 5:6]
                ts("dve", gsub, gsub, 1.0 - lam_init, ALU.mult, [t_gsub], [t_gsub])
            scale = (64 ** -0.5) if diff else (128 ** -0.5)
            nacc = 8 if diff else 4

            def acc_ap(idx):
                return psum[:, 4 + idx // 3, (idx % 3) * 130:(idx % 3) * 130 + 130], PB[4 + idx // 3]

            stage, t_stage = AR.alloc([128, 3, 390], F32)
            t_stg = [T(), T(), T()]

            def st_ap(idx):
                return stage[:, idx // 3, (idx % 3) * 130:(idx % 3) * 130 + 130]

            cnt = dict(sg=0, pb=0, st=0)
            for h in range(8):
                s = h % 2
                hk = h if diff else h // 4
                (KTs, t_KT), (VAs, t_VA), (QTs, t_QT) = KT[s], VA[s], QT[0]
                if h == 0:
                    dma("sp", KTs, kT_d[hk], [t_kTd], [t_KT])
                    dma("sp", VAs, v_d[:, :, hk, :], [t_vd], [t_VA])
                dma("sp", QTs, qT_d[h], [t_qTd], [t_QT])
                if h + 1 < 8:
                    hk1 = (h + 1) if diff else (h + 1) // 4
                    dma("sp", KT[1 - s][0], kT_d[hk1], [t_kTd], [KT[1 - s][1]])
                    dma("sp", VA[1 - s][0], v_d[:, :, hk1, :], [t_vd], [VA[1 - s][1]])
                chunks = [(qc * 512, 4, False) for qc in range(8)]
                if not last:
                    chunks.append((L, 2, True))
                allsteps = []
                for (q0, nq, isctx) in chunks:
                    kts = [32, 33] if isctx else list(range(NT))
                    if diff:
                        steps = [((kt, 0, 64), (kt, 64, 128)) for kt in kts]
                    else:
                        steps = [((kts[a_], 0, 128), (kts[a_ + 1], 0, 128)) for a_ in range(0, len(kts), 2)]
                    for si, (sa, sb2) in enumerate(steps):
                        allsteps.append(dict(q0=q0, nq=nq, sa=sa, sb=sb2, first=si == 0, last=si == len(steps) - 1))

                def emit_qk_exp(stp):
                    g = cnt["sg"] % 2
                    cnt["sg"] += 1
                    b0 = 2 * g
                    Wq = stp["nq"] * 128
                    q0 = stp["q0"]
                    for half, (kt, r0, r1) in enumerate((stp["sa"], stp["sb"])):
                        mm(psum[:, b0 + half, 0:Wq], KTs[r0:r1, kt * 128:(kt + 1) * 128], QTs[r0:r1, q0:q0 + Wq],
                           True, True, [t_KT, t_QT], [PB[b0 + half]])
                    P, t_P = Pb[cnt["pb"] % 2]
                    cnt["pb"] += 1
                    act(P[:, :, 0:Wq], psum[:, b0:b0 + 2, 0:Wq], AF.Exp, [PB[b0], PB[b0 + 1]], [t_P], scale=scale)
                    stp["P"] = (P, t_P)

                def emit_pv(stp):
                    P, t_P = stp["P"]
                    nq = stp["nq"]
                    for half, (kt, r0, r1) in enumerate((stp["sa"], stp["sb"])):
                        for jq in range(nq):
                            idx = (half * nq + jq) if diff else jq
                            a_ap, t_a = acc_ap(idx)
                            mm(a_ap, P[:, half, jq * 128:(jq + 1) * 128], VAs[:, kt, :], False, False,
                               [t_P, t_VA], [t_a], skip=True)

                def emit_evac(stp):
                    nq = stp["nq"]
                    q0 = stp["q0"]
                    Wq = nq * 128
                    nb = 3 if (diff and nq == 4) else 2
                    for b_ in range(3):
                        if b_ < nb:
                            copy("dve", stage[:, b_, :], psum[:, 4 + b_, 0:390], [PB[4 + b_]], [t_stg[b_]])
                        S.op("dve", lambda b_=b_: nc.vector.memset(psum[:, 4 + b_, 0:390], 0.0), writes=[PB[4 + b_]])
                    sts = [next_stat() for _ in range(nq)]
                    if diff:
                        for jq in range(nq):
                            st, t_st = sts[jq]
                            recip(st[:, 0:1], st_ap(jq)[:, 128:129], [t_stg[jq // 3]], [t_st])
                            recip(st[:, 1:2], st_ap(nq + jq)[:, 128:129], [t_stg[(nq + jq) // 3]], [t_st])
                        for jq in range(nq):
                            st, t_st = sts[jq]
                            tt("dve", st[:, 1:2], st[:, 1:2], nlam, ALU.mult, [t_st, t_lst], [t_st])
                        for jq in range(nq):
                            st, t_st = sts[jq]
                            ts("dve", o32[jq][0], st_ap(jq)[:, 0:128], st[:, 0:1], ALU.mult, [t_stg[jq // 3], t_st], [o32[jq][1]])
                        for jq in range(nq):
                            st, t_st = sts[jq]
                            stt("dve", o32[jq][0], st_ap(nq + jq)[:, 0:128], st[:, 1:2], o32[jq][0], ALU.mult, ALU.add,
                                [t_stg[(nq + jq) // 3], t_st, o32[jq][1]], [o32[jq][1]])

                        def stage2():
                            for jq in range(nq):
                                st, t_st = sts[jq]
                                act(junk[:, jq * 128:(jq + 1) * 128], o32[jq][0], AF.Square, [o32[jq][1]], [t_junk, t_st],
                                    accum=st[:, 2:3])
                            for jq in range(nq):
                                st, t_st = sts[jq]
                                act(st[:, 3:4], st[:, 2:3], AF.Ln, [t_st], [t_st], scale=1.0 / 128, bias=EPS)
                            for jq in range(nq):
                                st, t_st = sts[jq]
                                act(st[:, 3:4], st[:, 3:4], AF.Exp, [t_st], [t_st], scale=-0.5)

                        def stage3():
                            for jq in range(nq):
                                st, t_st = sts[jq]
                                stt("dve", obf[jq][0], o32[jq][0], st[:, 3:4], gsub, ALU.mult, ALU.mult,
                                    [o32[jq][1], t_st, t_gsub], [obf[jq][1]])
                    else:
                        for jq in range(nq):
                            st, t_st = sts[jq]
                            recip(st[:, 0:1], st_ap(jq)[:, 128:129], [t_stg[jq // 3]], [t_st])
                        for jq in range(nq):
                            st, t_st = sts[jq]
                            ts("dve", obf[jq][0], st_ap(jq)[:, 0:128], st[:, 0:1], ALU.mult, [t_stg[jq // 3], t_st], [obf[jq][1]])

                        def stage2():
                            pass

                        def stage3():
                            pass

                    def stage4():
                        pb7 = bank_bf(7)
                        for jq in range(nq):
                            tr(pb7[:, jq * 128:(jq + 1) * 128], obf[jq][0], [obf[jq][1]], [PB[7]])

                    def stage5():
                        pb7 = bank_bf(7)
                        ot, t_ot = OTst[cnt["st"] % 2]
                        cnt["st"] += 1
                        copy("act", ot[:, 0:Wq], pb7[:, 0:Wq], [PB[7]], [t_ot])
                        dma("pool", oT_d[h, :, q0:q0 + Wq], ot[:, 0:Wq], [t_ot], [t_oTd])
                    return [[3, stage2], [5, stage3], [7, stage4], [9, stage5]]

                deferred = []
                S.op("dve", lambda: nc.vector.memset(psum[:, 4:7, 0:390], 0.0), writes=[PB[4], PB[5], PB[6]])
                emit_qk_exp(allsteps[0])
                for n_, stp in enumerate(allsteps):
                    if n_ + 1 < len(allsteps):
                        emit_qk_exp(allsteps[n_ + 1])
                    emit_pv(stp)
                    for d_ in deferred:
                        d_[0] -= 1
                    while deferred and (deferred[0][0] <= 0 or stp["last"]):
                        deferred.pop(0)[1]()
                    if stp["last"]:
                        deferred = emit_evac(stp)
                while deferred:
                    deferred.pop(0)[1]()
            S.barrier()
            AR.release(m0)

        def phase_post(i, wo_src, last, bias_src=None):
            m0 = AR.mark()
            wo, t_wo = AR.alloc([128, 8, D], BF16)
            wv = wo_src.rearrange("(k p) n -> p k n", p=128)
            for k in range(0, 8, 4):
                dma("pool", wo[:, k:k + 4, :], wv[:, k:k + 4, :], [], [t_wo])
            nog = 3 if bias_src is None else 2
            OTg = [AR.alloc([128, 8, 256], BF16) for _ in range(nog)]
            xg = [AR.alloc([128, 2, D], F32) for _ in range(3)]
            t32s = [AR.alloc([128, D], F32) for _ in range(2)]
            if bias_src is not None:
                gbo, t_gbo = AR.alloc([128, D], F32)
            groups = [(2 * g_, 2) for g_ in range(NT // 2)]
            if last:
                groups = [g_ for g_ in groups if g_[0] < 32]

            def load_og(idx):
                if idx >= len(groups):
                    return
                t0, gs = groups[idx]
                og, t_og = OTg[idx % nog]
                c0 = t0 * 128
                dma("sp", og[:, :, 0:gs * 128], oT_d[:, :, c0:c0 + gs * 128].rearrange("h p t -> p h t"), [t_oTd], [t_og])

            def load_x(idx):
                if idx >= len(groups):
                    return
                t0, gs = groups[idx]
                xgb, t_xg = xg[idx % 3]
                dma("sp", xgb[:, 0:gs, :], xsrc(i, t0, gs), t_xs[t0:t0 + gs], [t_xg])

            for k_ in range(nog - 1):
                load_og(k_)
            load_x(0)
            load_x(1)
            ti = 0
            for idx, (t0, gs) in enumerate(groups):
                isctx = t0 >= 32
                if idx == 0 or isctx:
                    row = 1 if isctx else 0
                    dma("sp", gb, modsd[row, 2 * D:3 * D].partition_broadcast(128), [t_modsd], [t_gb])
                    if bias_src is not None:
                        dma("sp", gbo, bias_src.partition_broadcast(128), [], [t_gbo])
                        tt("dve", gbo, gbo, gb, ALU.mult, [t_gbo, t_gb], [t_gbo])
                load_og(idx + nog - 1)
                load_x(idx + 2)
                Wd = gs * 128
                c0 = t0 * 128
                og, t_og = OTg[idx % nog]
                xgb, t_xg = xg[idx % 3]
                for jj in range(gs):
                    b0 = 2 * (ti % 2)
                    t32, t_t32 = t32s[ti % 2]
                    ti += 1
                    for n in range(2):
                        for h in range(8):
                            mm(psum[:, b0 + n, :], og[:, h, jj * 128:(jj + 1) * 128], wo[:, h, n * 512:(n + 1) * 512],
                               h == 0, h == 7, [t_og, t_wo], [PB[b0 + n]])
                    tt("dve", t32, psum[:, b0:b0 + 2, :].rearrange("p a b -> p (a b)"), gb, ALU.mult,
                       [PB[b0], PB[b0 + 1], t_gb], [t_t32])
                    if bias_src is not None:
                        tt("dve", t32, t32, gbo, ALU.add, [t_t32, t_gbo], [t_t32])
                    tt("pool", xgb[:, jj, :], xgb[:, jj, :], t32, ALU.add, [t_xg, t_t32], [t_xg])
                dma("pool", xs[c0:c0 + Wd, :].rearrange("(j p) d -> p j d", p=128), xgb[:, 0:gs, :], [t_xg], t_xs[t0:t0 + gs])
            S.barrier()
            AR.release(m0)

        def phase_ffn(i, last):
            m0 = AR.mark()
            wgu, t_wgu = ffn_w["gu"]
            wdn, t_wdn = ffn_w["dn"]
            xn = [AR.alloc([128, D], F32) for _ in range(2)]
            hbf4 = [AR.alloc([128, D], BF16) for _ in range(4)]
            xr = [AR.alloc([128, D], F32) for _ in range(1 if last else 2)]
            t32, t_t32 = AR.alloc([128, D], F32)
            hT, t_hT = AR.alloc([128, 8, 512], BF16)
            actT, t_actT = AR.alloc([128, 22, 512], BF16)
            sg = [AR.alloc([128, 512], F32)]
            if last:
                fnb, t_fnb = AR.alloc([128, D], F32)
                dma("sp", fnb, final_norm.partition_broadcast(128), [], [t_fnb])
            groups = [g_ for g_ in GROUPS if not (last and g_[0] >= 32)]
            cnt = dict(xn=0, ti=0, ui=0, di=0)
            xn_of = {}

            def xs_tile(t):
                return xs[t * 128:(t + 1) * 128, :]

            def loadn(gidx, jj):
                t0, gs = groups[gidx]
                buf = xn[cnt["xn"] % 2]
                cnt["xn"] += 1
                xn_of[(gidx, jj)] = buf
                dma("sp", buf[0], xs_tile(t0 + jj), [t_xs[t0 + jj]], [buf[1]])

            def compA(gidx, jj):
                x_ap, t_x = xn_of[(gidx, jj)]
                norm_mod_A(x_ap, t_x, t32, t_t32, hbf4[jj][0], hbf4[jj][1])

            def partB(gidx):
                t0, gs = groups[gidx]
                for jj in range(gs):
                    norm_mod_B(hbf4[jj][0], hbf4[jj][1], cnt["ti"] % 2, hT[:, :, jj * 128:(jj + 1) * 128], t_hT)
                    cnt["ti"] += 1

            prep_mod(i, 2, 0, (t32, t_t32))
            gs0 = groups[0][1]
            loadn(0, 0)
            loadn(0, 1)
            for jj in range(gs0):
                compA(0, jj)
                if jj + 2 < gs0:
                    loadn(0, jj + 2)
            partB(0)
            for gidx, (t0, gs) in enumerate(groups):
                Wd = gs * 128
                nxt = gidx + 1 < len(groups)
                sched = {}
                if nxt:
                    t1, gs1 = groups[gidx + 1]
                    nctx = t1 >= 32

                    def s0(gidx=gidx, nctx=nctx):
                        if nctx:
                            prep_mod(i, 2, 1, (t32, t_t32), need_g=False)
                        loadn(gidx + 1, 0)
                        loadn(gidx + 1, 1)
                    sched[0] = s0
                    for q_, m_ in enumerate((4, 8, 12, 16)):
                        if q_ < gs1:
                            def sA(gidx=gidx, q_=q_, gs1=gs1):
                                compA(gidx + 1, q_)
                                if q_ + 2 < gs1:
                                    loadn(gidx + 1, q_ + 2)
                            sched[m_] = sA
                nxr = len(xr)

                def loadr(jj, t0=t0):
                    buf = xr[jj % nxr]
                    dma("sp", buf[0], xs_tile(t0 + jj), [t_xs[t0 + jj]], [buf[1]])

                for m in range(22):
                    if m in sched:
                        sched[m]()
                    if m == 10:
                        for jj in range(min(nxr, gs)):
                            loadr(jj)
                    bg = 2 + 2 * (cnt["ui"] % 2)
                    cnt["ui"] += 1
                    for k in range(8):
                        mm(psum[:, bg, 0:Wd], wgu[:, k, m * 128:(m + 1) * 128], hT[:, k, 0:Wd], k == 0, k == 7,
                           [t_wgu, t_hT], [PB[bg]])
                    for k in range(8):
                        mm(psum[:, bg + 1, 0:Wd], wgu[:, k, DFF + m * 128:DFF + (m + 1) * 128], hT[:, k, 0:Wd], k == 0, k == 7,
                           [t_wgu, t_hT], [PB[bg + 1]])
                    s_, t_s = sg[0]
                    act(s_[:, 0:Wd], psum[:, bg, 0:Wd], AF.Silu, [PB[bg]], [t_s])
                    tt("dve", actT[:, m, 0:Wd], psum[:, bg + 1, 0:Wd], s_[:, 0:Wd], ALU.mult, [PB[bg + 1], t_s], [t_actT])
                for jj in range(gs):
                    xr_ap, t_xr = xr[jj % nxr]
                    for n in range(2):
                        bd = 6 + (cnt["di"] % 2)
                        cnt["di"] += 1
                        for m in range(22):
                            mm(psum[:, bd, :], actT[:, m, jj * 128:(jj + 1) * 128], wdn[:, m, n * 512:(n + 1) * 512],
                               m == 0, m == 21, [t_actT, t_wdn], [PB[bd]])
                        tt("dve", t32[:, n * 512:(n + 1) * 512], psum[:, bd, :], gb[:, n * 512:(n + 1) * 512], ALU.mult,
                           [PB[bd], t_gb], [t_t32])
                    tt("pool", xr_ap, xr_ap, t32, ALU.add, [t_xr, t_t32], [t_xr])
                    if last:
                        r, t_r = rms_rstd(xr_ap, t_xr, D)
                        stt("dve", xr_ap, xr_ap, r, fnb, ALU.mult, ALU.mult, [t_xr, t_r, t_fnb], [t_xr])
                    dst = out if last else xs
                    dma("pool", dst[(t0 + jj) * 128:(t0 + jj + 1) * 128, :], xr_ap, [t_xr],
                        [T()] if last else [t_xs[t0 + jj]])
                    if jj + nxr < gs:
                        loadr(jj + nxr)
                    if nxt and jj == min(1, gs - 1):
                        partB(gidx + 1)
                if nxt and groups[gidx + 1][0] >= 32:
                    dma("sp", gb, modsd[1, 5 * D:6 * D].partition_broadcast(128), [t_modsd], [t_gb])
            S.barrier()
            AR.release(m0)

        def phase_fnet(i, last):
            m0 = AR.mark()
            tabc, t_tabc = AR.alloc([128, 3, 2, 256], BF16)
            dma("sp", tabc, dft256_in.rearrange("a p k n -> p a k n"), [], [t_tabc])
            xg = [AR.alloc([128, 4, D], F32) for _ in range(2)]
            t32, t_t32 = AR.alloc([128, D], F32)
            hbfs = [AR.alloc([128, D], BF16) for _ in range(2)]
            hTs = [AR.alloc([128, 8, 512], BF16) for _ in range(2)]
            Gst = [AR.alloc([128, 4, 2048], BF16) for _ in range(2)]
            ti = 0
            f1groups = [g_ for g_ in GROUPS if not (last and g_[0] >= 32)]

            def f1load(gi):
                t0, gs = f1groups[gi]
                xgb, t_xg = xg[gi % 2]
                dma("sp", xgb[:, 0:gs, :], xsrc(i, t0, gs), t_xs[t0:t0 + gs], [t_xg])

            f1load(0)
            for gi, (t0, gs) in enumerate(f1groups):
                isctx = t0 >= 32
                if gi + 1 < len(f1groups):
                    f1load(gi + 1)
                if gi == 0 or isctx:
                    prep_mod(i, 1, 1 if isctx else 0, (t32, t_t32), need_g=False)
                Wd = gs * 128
                c0 = t0 * 128
                xgb, t_xg = xg[gi % 2]
                hT, t_hT = hTs[gi % 2]
                gst, t_gst = Gst[gi % 2]
                for jj in range(gs):
                    hbf, t_hbf = hbfs[ti % 2]
                    norm_mod_T(xgb[:, jj, :], t_xg, t32, t_t32, hbf, t_hbf, ti % 2, hT[:, :, jj * 128:(jj + 1) * 128], t_hT)
                    ti += 1
                for jj in range(gs):
                    for part in range(2):
                        for half in range(2):
                            b = 2 + part * 2 + half
                            for gg in range(2):
                                g = half * 2 + gg
                                for kk in range(2):
                                    mm(psum[:, b, gg * 256:(gg + 1) * 256], hT[:, 2 * g + kk, jj * 128:(jj + 1) * 128],
                                       tabc[:, part, kk, :], kk == 0, kk == 1, [t_hT, t_tabc], [PB[b]])
                    copy("act", gst[:, jj, :], psum[:, 2:6, :].rearrange("p a b -> p (a b)"),
                         [PB[2], PB[3], PB[4], PB[5]], [t_gst])
                dma("pool", g_d[c0:c0 + Wd, :].rearrange("(j p) n -> p j n", p=128), gst[:, 0:gs, :], [t_gst], [t_gd])
            S.barrier()
            AR.release(m0)
            m0 = AR.mark()
            tabc, t_tabc = AR.alloc([128, 3, 2, 256], BF16)
            dma("sp", tabc, dft256_in.rearrange("a p k n -> p a k n"), [], [t_tabc])
            Gh, t_Gh = AR.alloc([128, 32, 2, 512], BF16)
            Tb = [AR.alloc([128, 32, 2, 256], BF16) for _ in range(2)]
            fst = [AR.alloc([128, 4, 256], BF16) for _ in range(2)]
            gv = g_d[0:L, :].rearrange("(tc p) (ri c) -> p tc ri c", p=128, ri=2)
            dv = dftL_in.rearrange("cs (tc p) o -> p tc cs o", p=128)
            bi = 0
            tbi = 0

            def tload(k_):
                oc_ = k_ % 16
                tb_, t_tb = Tb[k_ % 2]
                for q2 in range(2):
                    for cs in range(2):
                        dma("sp", tb_[:, q2 * 16:(q2 + 1) * 16, cs, :], dv[:, q2 * 16:(q2 + 1) * 16, cs, oc_ * 256:(oc_ + 1) * 256],
                            [], [t_tb])

            tload(0)
            for half in range(2):
                for q4 in range(4):
                    for ri in range(2):
                        dma("sp", Gh[:, q4 * 8:(q4 + 1) * 8, ri, :], gv[:, q4 * 8:(q4 + 1) * 8, ri, half * 512:(half + 1) * 512],
                            [t_gd], [t_Gh])
                for oc in range(16):
                    tb_, t_tb = Tb[tbi % 2]
                    f_, t_f = fst[tbi % 2]
                    tbi += 1
                    if tbi < 32:
                        tload(tbi)
                    for m in range(4):
                        b = bi % 4
                        bi += 1
                        for tc in range(32):
                            for part in range(2):
                                mm(psum[:, b, 0:256], Gh[:, tc, part, m * 128:(m + 1) * 128], tb_[:, tc, part, :],
                                   tc == 0 and part == 0, tc == 31 and part == 1, [t_Gh, t_tb], [PB[b]])
                        copy("act" if m % 2 == 0 else "dve", f_[:, m, :], psum[:, b, 0:256], [PB[b]], [t_f])
                    dma("pool", oT_d[half * 4:(half + 1) * 4, :, oc * 256:(oc + 1) * 256].rearrange("h p t -> p h t"), f_,
                        [t_f], [t_oTd])
            if not last:
                Gc, t_Gc = AR.alloc([128, 2, 2, D], BF16)
                fc, t_fc = AR.alloc([128, 8, 256], BF16)
                dma("sp", Gc, g_d[L:NTOK, :].rearrange("(tc p) (ri c) -> p tc ri c", p=128, ri=2), [t_gd], [t_Gc])
                for m in range(8):
                    b = bi % 4
                    bi += 1
                    for tc in range(2):
                        for part in range(2):
                            tabsel = 0 if part == 0 else 2
                            mm(psum[:, b, 0:256], Gc[:, tc, part, m * 128:(m + 1) * 128], tabc[:, tabsel, tc, :],
                               tc == 0 and part == 0, tc == 1 and part == 1, [t_Gc, t_tabc], [PB[b]])
                    copy("act" if m % 2 == 0 else "dve", fc[:, m, :], psum[:, b, 0:256], [PB[b]], [t_fc])
                dma("sp", oT_d[:, :, L:NTOK].rearrange("h p t -> p h t"), fc, [t_fc], [t_oTd])
            S.barrier()
            AR.release(m0)

        for i in layers:
            last = i == DEPTH - 1
            kind, j = i % 3, i // 3
            S.new_epoch()
            phase_mod(i)
            if kind in (0, 1):
                phase_qkv(i, kind, j, last)
                mL = AR.mark()
                load_ffn_weights(i)
                phase_att(i, kind, j, last)
                S.new_epoch()
                wo_src = a_w_o[j] if kind == 0 else b_w_o[j]
                phase_post(i, wo_src, last)
            else:
                phase_fnet(i, last)
                mL = AR.mark()
                load_ffn_weights(i)
                S.new_epoch()
                phase_post(i, c_w_o[j], last, bias_src=c_b_o[j, :])
            phase_ffn(i, last)
            AR.release(mL)
        stats = S.emit()
    return nc, stats


_CONST = {}


def _consts():
    if _CONST:
        return _CONST
    bf = ml_dtypes.bfloat16
    _CONST["ident"] = np.eye(128, dtype=np.float32).astype(bf)
    t = np.arange(L)
    rows = (t // 64).astype(np.float32)
    cols = (t % 64).astype(np.float32)

    def rope(head_dim):
        axis_dim = head_dim // 2
        inv = (10000.0 ** (-np.arange(0, axis_dim, 2, dtype=np.float32) / axis_dim)).astype(np.float32)
        ang = np.concatenate([rows[:, None] * inv, cols[:, None] * inv], axis=-1).astype(np.float32)
        cs = np.stack([np.cos(ang), np.sin(ang)]).astype(np.float32)
        return np.ascontiguousarray(cs.reshape(2, 32, 128, -1).transpose(0, 2, 1, 3))

    _CONST["ropeA"] = rope(64)
    _CONST["ropeB"] = rope(128)
    k = np.arange(L, dtype=np.int64)
    m = (k[:, None] * k[None, :]) % L
    ang = (2.0 * np.pi / L) * m.astype(np.float64)
    dl = np.empty((2, L, L), dtype=bf)
    dl[0] = (np.cos(ang) / 64.0).astype(np.float32).astype(bf)
    dl[1] = (np.sin(ang) / 64.0).astype(np.float32).astype(bf)
    _CONST["dftL"] = dl
    k2 = np.arange(256, dtype=np.int64)
    a2 = (2.0 * np.pi / 256) * ((k2[:, None] * k2[None, :]) % 256).astype(np.float64)
    c2 = (np.cos(a2) / 16.0)
    s2 = (np.sin(a2) / 16.0)
    tab = np.stack([c2, -s2, s2]).astype(np.float32).astype(bf)
    _CONST["dft256"] = np.ascontiguousarray(tab.reshape(3, 2, 128, 256).transpose(0, 2, 1, 3))
    return _CONST


_NC_CACHE = {}


def _get_nc(layers=(0, 1, 2, 3)):
    key = tuple(layers)
    if key not in _NC_CACHE:
        _NC_CACHE[key] = build_nc(layers)
    return _NC_CACHE[key]


def make_in_maps(inputs):
    cst = _consts()
    f = lambda a: np.ascontiguousarray(np.asarray(a, dtype=np.float32))
    shared = {
        "mod_w": f(inputs["mod_w"]), "mod_b": f(inputs["mod_b"]),
        "ln_mix": f(inputs["ln_mix"]), "ln_ffn": f(inputs["ln_ffn"]),
        "ffn_w_gu": f(inputs["ffn_w_gu"]), "ffn_w_down": f(inputs["ffn_w_down"]),
        "a_w_qkv": f(inputs["a_w_qkv"]), "a_lam": f(inputs["a_lam"]).reshape(2, 256),
        "a_subln": f(inputs["a_subln"]), "a_w_o": f(inputs["a_w_o"]),
        "b_w_qkv": f(inputs["b_w_qkv"]), "b_q_norm": f(inputs["b_q_norm"]), "b_k_norm": f(inputs["b_k_norm"]),
        "b_w_o": f(inputs["b_w_o"]), "c_w_o": f(inputs["c_w_o"]), "c_b_o": f(inputs["c_b_o"]),
        "final_norm": f(inputs["final_norm"]),
        "ident": cst["ident"], "ropeA": cst["ropeA"], "ropeB": cst["ropeB"],
        "dftL": cst["dftL"], "dft256": cst["dft256"],
    }
    x = f(inputs["x"])
    ctx = f(inputs["ctx"])
    c = f(inputs["c"])
    cc = f(inputs["c_ctx"])
    maps = []
    for b in range(8):
        ccol = np.ascontiguousarray(np.stack([c[b].reshape(8, 128).T, cc.reshape(8, 128).T], axis=-1))
        d = dict(shared)
        d["x"] = x[b]
        d["ctx"] = ctx[b]
        d["ccol"] = ccol
        maps.append(d)
    return maps


def kernel(**inputs):
    nc, _ = _get_nc()
    maps = make_in_maps(inputs)
    res = run_bass_kernel_spmd(nc, maps, core_ids=list(range(8)))
    return np.stack([np.asarray(r["out"], dtype=np.float32) for r in res.results], axis=0)
```
